# Optimizing a Trainium2 kernel written in Bass

```python
import jax, jax.numpy as jnp
from jax import lax
import numpy as np

D_MODEL = 1024
BATCH = 32
SEQ = 2048
DEPTH = 2

HEAD_DIM = 64
N_MIX_HEADS = D_MODEL // HEAD_DIM
A_Q_HEADS = N_MIX_HEADS // 2
A_KV_HEADS = 2
B_Q_HEADS = N_MIX_HEADS - A_Q_HEADS
B_KV_HEADS = 2
MIX_WIDTH = (A_Q_HEADS + B_Q_HEADS) * HEAD_DIM
QA_W = A_Q_HEADS * HEAD_DIM
KVA_W = A_KV_HEADS * HEAD_DIM
QB_W = B_Q_HEADS * HEAD_DIM
KVB_W = B_KV_HEADS * HEAD_DIM
IN_WIDTH = QA_W + 2 * KVA_W + QB_W + 2 * KVB_W
SPLITS = [QA_W, QA_W + KVA_W, QA_W + 2 * KVA_W, QA_W + 2 * KVA_W + QB_W,
          QA_W + 2 * KVA_W + QB_W + KVB_W]

BLOCK = 128
WINDOW = 128
GRID_W = 64
ROPE_THETA = 10000.0
ROPE_FREQS = HEAD_DIM // 4

PEER_HEADS = 8
PEER_NKEYS = 128
PEER_EXPERTS = PEER_NKEYS * PEER_NKEYS
PEER_DKEY = 256
PEER_TOPK = 16
PEER_CHUNK = 128

DEEPNORM_ALPHA = (2.0 * DEPTH) ** 0.25
DEEPNORM_BETA = (8.0 * DEPTH) ** -0.25
LN_EPS = 1e-5
RMS_EPS = 1e-6
NEG_INF = -1e30

kernel_name = 'hymba_axial_window_peer_encoder'


def alibi_slopes(n):
    return (2.0 ** (-(np.arange(1, n + 1, dtype=np.float32) * 8.0 / n))).astype(np.float32)


def layer_norm(x, g, b):
    xf = x.astype(jnp.float32)
    mu = xf.mean(-1, keepdims=True)
    var = jnp.square(xf - mu).mean(-1, keepdims=True)
    return ((xf - mu) * lax.rsqrt(var + LN_EPS) * g + b).astype(x.dtype)


def rms_norm(x, g):
    xf = x.astype(jnp.float32)
    return (xf * lax.rsqrt(jnp.square(xf).mean(-1, keepdims=True) + RMS_EPS) * g).astype(x.dtype)


def axial_rope_tables(S):
    rows = S // GRID_W
    row = jnp.repeat(jnp.arange(rows), GRID_W)
    col = jnp.tile(jnp.arange(GRID_W), rows)
    pos = jnp.stack([row, col], axis=-1).astype(jnp.float32)
    inv_freq = ROPE_THETA ** (-jnp.arange(ROPE_FREQS, dtype=jnp.float32) / ROPE_FREQS)
    ang = pos[:, :, None] * inv_freq
    return jnp.cos(ang), jnp.sin(ang)


def apply_axial_rope(x, cos, sin):
    B, S, H, dh = x.shape
    xr = x.astype(jnp.float32).reshape(B, S, H, 2, 2, ROPE_FREQS)
    x1, x2 = xr[..., 0, :], xr[..., 1, :]
    c, s = cos[None, :, None], sin[None, :, None]
    out = jnp.stack([x1 * c - x2 * s, x1 * s + x2 * c], axis=-2)
    return out.reshape(B, S, H, dh).astype(x.dtype)


def global_attention(q, k, v):
    B, S, Hq, dh = q.shape
    Hkv = k.shape[2]
    G = Hq // Hkv
    nb = S // BLOCK
    scale = dh ** -0.5
    qb = q.reshape(B, nb, BLOCK, Hkv, G, dh).transpose(1, 0, 2, 3, 4, 5)

    def one_block(qblk):
        sc = jnp.einsum('bqkgd,bskd->bkgqs', qblk, k, preferred_element_type=jnp.float32) * scale
        p = jax.nn.softmax(sc, axis=-1).astype(v.dtype)
        return jnp.einsum('bkgqs,bskd->bqkgd', p, v)

    o = lax.map(one_block, qb)
    return o.transpose(1, 0, 2, 3, 4, 5).reshape(B, S, Hq, dh)


def window_sink_attention(q, k, v, sink, slopes):
    B, S, Hq, dh = q.shape
    Hkv = k.shape[2]
    G = Hq // Hkv
    nb = S // BLOCK
    span = BLOCK + 2 * WINDOW
    scale = dh ** -0.5
    qb = q.reshape(B, nb, BLOCK, Hkv, G, dh).transpose(1, 0, 2, 3, 4, 5)
    pad = ((0, 0), (WINDOW, WINDOW), (0, 0), (0, 0))
    kp = jnp.pad(k, pad)
    vp = jnp.pad(v, pad)
    slope = slopes.reshape(Hkv, G, 1, 1)
    sink_l = sink.astype(jnp.float32).reshape(Hkv, G, 1)
    q_off = jnp.arange(BLOCK)
    k_off = jnp.arange(span) - WINDOW

    def one_block(args):
        qblk, j = args
        start = j * BLOCK
        kb = lax.dynamic_slice_in_dim(kp, start, span, axis=1)
        vb = lax.dynamic_slice_in_dim(vp, start, span, axis=1)
        t = start + q_off
        s_pos = start + k_off
        dist = jnp.abs(t[:, None] - s_pos[None, :])
        valid = (dist <= WINDOW) & (s_pos >= 0)[None, :] & (s_pos < S)[None, :]
        sc = jnp.einsum('bqkgd,bskd->bkgqs', qblk, kb, preferred_element_type=jnp.float32) * scale
        sc = sc - slope * dist.astype(jnp.float32)
        sc = jnp.where(valid, sc, NEG_INF)
        m = jnp.maximum(sc.max(-1), sink_l)
        p = jnp.exp(sc - m[..., None])
        denom = p.sum(-1) + jnp.exp(sink_l - m)
        p = (p / denom[..., None]).astype(v.dtype)
        return jnp.einsum('bkgqs,bskd->bqkgd', p, vb)

    o = lax.map(one_block, (qb, jnp.arange(nb)))
    return o.transpose(1, 0, 2, 3, 4, 5).reshape(B, S, Hq, dh)


def peer_ffn(x, wq, sub_keys, u, v):
    B, S, D = x.shape
    xt = x.reshape(-1, PEER_CHUNK, D)
    half = PEER_DKEY // 2

    def chunk(xc):
        q = (xc @ wq).reshape(PEER_CHUNK, PEER_HEADS, 2, half)
        sc = jnp.einsum('chpd,hpnd->chpn', q, sub_keys, preferred_element_type=jnp.float32)
        s1, i1 = lax.top_k(sc[:, :, 0], PEER_TOPK)
        s2, i2 = lax.top_k(sc[:, :, 1], PEER_TOPK)
        cand = (s1[..., :, None] + s2[..., None, :]).reshape(PEER_CHUNK, PEER_HEADS, PEER_TOPK * PEER_TOPK)
        cidx = (i1[..., :, None] * PEER_NKEYS + i2[..., None, :]).reshape(PEER_CHUNK, PEER_HEADS, PEER_TOPK * PEER_TOPK)
        top_s, pos = lax.top_k(cand, PEER_TOPK)
        eidx = jnp.take_along_axis(cidx, pos, axis=-1)
        g = jax.nn.softmax(top_s, axis=-1)
        u_sel = jnp.take(u, eidx, axis=0)
        a = jax.nn.gelu(jnp.einsum('chkd,cd->chk', u_sel, xc, preferred_element_type=jnp.float32), approximate=False)
        w = (g * a).astype(x.dtype)
        v_sel = jnp.take(v, eidx, axis=0)
        return jnp.einsum('chk,chkd->cd', w, v_sel)

    return lax.map(chunk, xt).reshape(B, S, D)


def setup_inputs(seed: int = 0) -> dict:
    key = jax.random.key(seed)
    ks = jax.random.split(key, 24)
    L, D = DEPTH, D_MODEL
    beta = DEEPNORM_BETA

    def nrm(k, shape, std):
        return jax.random.normal(k, shape, jnp.float32) * std

    x = nrm(ks[0], (BATCH, SEQ, D), 1.0)
    ln_in_g = 1.0 + nrm(ks[1], (D,), 0.02)
    ln_in_b = nrm(ks[2], (D,), 0.02)
    col_scale = jnp.concatenate([
        jnp.ones((QA_W + KVA_W,), jnp.float32), jnp.full((KVA_W,), beta, jnp.float32),
        jnp.ones((QB_W + KVB_W,), jnp.float32), jnp.full((KVB_W,), beta, jnp.float32)])
    w_in = nrm(ks[3], (L, D, IN_WIDTH), D ** -0.5) * col_scale
    qn_g = 1.0 + nrm(ks[4], (L, HEAD_DIM), 0.02)
    kn_g = 1.0 + nrm(ks[5], (L, HEAD_DIM), 0.02)
    sink = nrm(ks[6], (L, B_Q_HEADS), 0.5)
    gn_a_g = 1.0 + nrm(ks[7], (L, QA_W), 0.02)
    gn_b_g = 1.0 + nrm(ks[8], (L, QB_W), 0.02)
    w_o = nrm(ks[9], (L, MIX_WIDTH, D), beta * MIX_WIDTH ** -0.5)
    ln1_g = 1.0 + nrm(ks[10], (L, D), 0.02)
    ln1_b = nrm(ks[11], (L, D), 0.02)
    peer_wq = nrm(ks[12], (L, D, PEER_HEADS * PEER_DKEY), D ** -0.5)
    peer_keys = nrm(ks[13], (L, PEER_HEADS, 2, PEER_NKEYS, PEER_DKEY // 2), (PEER_DKEY // 2) ** -0.5)
    peer_u = nrm(ks[14], (L, PEER_EXPERTS, D), D ** -0.5)
    peer_v = nrm(ks[15], (L, PEER_EXPERTS, D), beta * (PEER_HEADS * PEER_TOPK) ** -0.5)
    ln2_g = 1.0 + nrm(ks[16], (L, D), 0.02)
    ln2_b = nrm(ks[17], (L, D), 0.02)
    return {'x': x, 'ln_in_g': ln_in_g, 'ln_in_b': ln_in_b, 'w_in': w_in,
            'qn_g': qn_g, 'kn_g': kn_g, 'sink': sink, 'gn_a_g': gn_a_g, 'gn_b_g': gn_b_g,
            'w_o': w_o, 'ln1_g': ln1_g, 'ln1_b': ln1_b, 'peer_wq': peer_wq,
            'peer_keys': peer_keys, 'peer_u': peer_u, 'peer_v': peer_v,
            'ln2_g': ln2_g, 'ln2_b': ln2_b}


def reference(x, ln_in_g, ln_in_b, w_in, qn_g, kn_g, sink, gn_a_g, gn_b_g, w_o,
              ln1_g, ln1_b, peer_wq, peer_keys, peer_u, peer_v, ln2_g, ln2_b):
    B, S, D = x.shape
    cos, sin = axial_rope_tables(S)
    slopes = jnp.asarray(alibi_slopes(B_Q_HEADS))
    h = layer_norm(x, ln_in_g, ln_in_b)
    for l in range(DEPTH):
        proj = h @ w_in[l]
        qa, ka, va, qb, kb, vb = jnp.split(proj, SPLITS, axis=-1)
        qa = qa.reshape(B, S, A_Q_HEADS, HEAD_DIM)
        ka = ka.reshape(B, S, A_KV_HEADS, HEAD_DIM)
        va = va.reshape(B, S, A_KV_HEADS, HEAD_DIM)
        qb = qb.reshape(B, S, B_Q_HEADS, HEAD_DIM)
        kb = kb.reshape(B, S, B_KV_HEADS, HEAD_DIM)
        vb = vb.reshape(B, S, B_KV_HEADS, HEAD_DIM)
        qa = apply_axial_rope(rms_norm(qa, qn_g[l]), cos, sin)
        ka = apply_axial_rope(rms_norm(ka, kn_g[l]), cos, sin)
        oa = global_attention(qa, ka, va)
        ob = window_sink_attention(qb, kb, vb, sink[l], slopes)
        oa = rms_norm(oa, gn_a_g[l].reshape(A_Q_HEADS, HEAD_DIM)).reshape(B, S, QA_W)
        ob = rms_norm(ob, gn_b_g[l].reshape(B_Q_HEADS, HEAD_DIM)).reshape(B, S, QB_W)
        mix = jnp.concatenate([oa, ob], axis=-1) @ w_o[l]
        h = layer_norm(DEEPNORM_ALPHA * h + mix, ln1_g[l], ln1_b[l])
        ffn = peer_ffn(h, peer_wq[l], peer_keys[l], peer_u[l], peer_v[l])
        h = layer_norm(DEEPNORM_ALPHA * h + ffn, ln2_g[l], ln2_b[l])
    return h
```

```python
from contextlib import ExitStack

import concourse.bass as bass
import concourse.mybir as mybir

F32 = mybir.dt.float32
BF16 = mybir.dt.bfloat16
I32 = mybir.dt.int32
U32 = mybir.dt.uint32
ALU = mybir.AluOpType
AF = mybir.ActivationFunctionType
AX = mybir.AxisListType

EPOCH = 30000


class Eng:
    def __init__(self, prog, name, h, same_engine_sync=True):
        self.prog = prog
        self.name = name
        self.h = h
        self.sems = []
        self.count = 0
        self.waited = {}
        self.same = same_engine_sync
        self.ninst = 0
        self._new_epoch()

    def _new_epoch(self):
        s = self.prog.stack.enter_context(
            self.prog.nc.semaphore(f"s_{self.name}_{len(self.sems)}"))
        self.sems.append(s)
        self.sem = s
        self.count = 0

    def wait(self, sem, val):
        if self.waited.get(id(sem), 0) < val:
            self.h.wait_ge(sem, val)
            self.waited[id(sem)] = val
            self.ninst += 1


class Buf:
    def __init__(self, prog, t, name):
        self.prog = prog
        self.t = t
        self.name = name
        self.w = {}
        self.r = {}
        self.dslot = None
        self.stack = None

    def __getitem__(self, idx):
        return self.t[idx]

    def ap(self):
        return self.t[:]

    def _dslot(self):
        if self.dslot is None:
            pr = self.prog
            if pr.dfree:
                self.dslot = pr.dfree.pop()
            else:
                sem = pr.stack.enter_context(pr.nc.semaphore(f"d_{len(pr.dslots)}"))
                self.dslot = [sem, 0]
                pr.dslots.append(self.dslot)
            if self.stack is not None and self.stack is not pr.stack:
                slot = self.dslot
                self.stack.callback(lambda: pr.dfree.append(slot))
        return self.dslot


def _merge(deps, entries, skip_eng=None, skip_sem=None):
    for k, (sem, val, en) in entries.items():
        if skip_eng is not None and en == skip_eng:
            continue
        if skip_sem is not None and sem is skip_sem:
            continue
        if k not in deps or deps[k][1] < val:
            deps[k] = (sem, val, en)


class Prog:
    def __init__(self, nc, stack):
        self.nc = nc
        self.stack = stack
        self.dslots = []
        self.dfree = []
        self.pe = Eng(self, "pe", nc.tensor, same_engine_sync=False)
        self.dve = Eng(self, "dve", nc.vector)
        self.act = Eng(self, "act", nc.scalar)
        self.pool = Eng(self, "pool", nc.gpsimd)
        self.sp = Eng(self, "sp", nc.sync)
        self.engs = [self.pe, self.dve, self.act, self.pool, self.sp]
        self.nbuf = 0

    def sbuf(self, name, shape, dtype, stack=None):
        st = stack or self.stack
        t = st.enter_context(self.nc.sbuf_tensor(f"{name}_{self.nbuf}", list(shape), dtype))
        self.nbuf += 1
        b = Buf(self, t, name)
        b.stack = st
        return b

    def psum(self, name, shape, dtype=F32, stack=None):
        st = stack or self.stack
        t = st.enter_context(self.nc.psum_tensor(f"{name}_{self.nbuf}", list(shape), dtype))
        self.nbuf += 1
        return Buf(self, t, name)

    def op(self, eng, fn, reads=(), writes=()):
        deps = {}
        for b in reads:
            _merge(deps, b.w)
        for b in writes:
            _merge(deps, b.w)
            _merge(deps, b.r, skip_eng=eng.name)
        for k, (sem, val, en) in deps.items():
            if en == eng.name and not eng.same:
                continue
            eng.wait(sem, val)
        if eng.count >= EPOCH:
            eng._new_epoch()
        inst = fn(eng.h)
        eng.count += 1
        eng.ninst += 1
        inst.then_inc(eng.sem, 1)
        ent = (eng.sem, eng.count, eng.name)
        for b in reads:
            b.r[id(eng.sem)] = ent
        for b in writes:
            b.w = {id(eng.sem): ent}
            b.r = {}
        return inst

    def dma(self, q, fn, dst=None, src=None, extra_reads=()):
        own = dst if dst is not None else src
        slot = own._dslot()
        dsem = slot[0]
        deps = {}
        if dst is not None:
            _merge(deps, dst.w, skip_sem=dsem)
            _merge(deps, dst.r)
        if src is not None:
            _merge(deps, src.w)
        for b in extra_reads:
            _merge(deps, b.w)
        for k, (sem, val, en) in deps.items():
            q.wait(sem, val)
        inst = fn(q.h)
        q.ninst += 1
        slot[1] += 1
        inst.then_inc(dsem, 16)
        ent = (dsem, 16 * slot[1], "dma")
        if dst is not None:
            dst.w[id(dsem)] = ent
            dst.r = {}
        if src is not None:
            src.r[id(dsem)] = ent
        for b in extra_reads:
            b.r[id(dsem)] = ent
        return inst

    def barrier(self):
        for e in self.engs:
            for o in self.engs:
                if o is e:
                    continue
                if o.count == 0:
                    if len(o.sems) > 1:
                        e.wait(o.sems[-2], EPOCH)
                    continue
                e.wait(o.sem, o.count)
            for sl in self.dslots:
                if sl[1]:
                    e.wait(sl[0], 16 * sl[1])

    def finish(self):
        for sl in self.dslots:
            if sl[1]:
                self.sp.wait(sl[0], 16 * sl[1])
        for o in self.engs:
            if o is self.sp:
                continue
            if o.count == 0:
                if len(o.sems) > 1:
                    self.sp.wait(o.sems[-2], EPOCH)
                continue
            self.sp.wait(o.sem, o.count)

    def ninst(self):
        return sum(e.ninst for e in self.engs)

import numpy as np
from concourse.bass_utils import run_bass_kernel_spmd

P = 128
D = 1024
S = 2048
NT = 16
NL = 2
ALPHA = (2.0 * NL) ** 0.25
LN_EPS = 1e-5
RMS_EPS = 1e-6
NCORES = 8


def build(nseq=4, n_layers=2, do_attn=True, do_peer=True, peer_slots=128, debug=False):
    nc = bass.Bass("TRN2", target_bir_lowering=False)

    def dt(name, shape, dtype=F32, kind="ExternalInput"):
        return nc.dram_tensor(name, list(shape), dtype, kind=kind).ap()

    x = dt("x", [nseq, S, D])
    out = dt("out", [nseq, S, D], kind="ExternalOutput")
    ln_in_g = dt("ln_in_g", [1, D]); ln_in_b = dt("ln_in_b", [1, D])
    w_in = dt("w_in", [NL, D, 1536])
    qn_g = dt("qn_g", [NL, 64]); kn_g = dt("kn_g", [NL, 64])
    sink = dt("sink", [NL, 8])
    gnT = dt("gnT", [NL, P, 8])
    w_o = dt("w_o", [NL, D, D])
    ln1_g = dt("ln1_g", [NL, D]); ln1_b = dt("ln1_b", [NL, D])
    peer_wq = dt("peer_wq", [NL, D, 2048])
    peer_keys = dt("peer_keys", [NL, 16, P, P])
    peer_u = [dt(f"peer_u{i}", [16384, D]) for i in range(NL)]; peer_v = [dt(f"peer_v{i}", [16384, D]) for i in range(NL)]
    ln2_g = dt("ln2_g", [NL, D]); ln2_b = dt("ln2_b", [NL, D])
    c_ident = dt("c_ident", [P, P])
    c_cos = dt("c_cos", [P, NT, 32]); c_sin = dt("c_sin", [P, NT, 32])
    c_negd = dt("c_negd", [P, 384])
    c_iota = dt("c_iota", [P, 16])
    if debug:
        dbg = dict(sc=dt("dbg_sc", [P, 2048], F32, "ExternalOutput"), eidx=dt("dbg_eidx", [P, 128], I32, "ExternalOutput"),
                   gw=dt("dbg_gw", [P, 128], F32, "ExternalOutput"), adot=dt("dbg_adot", [P, 128], F32, "ExternalOutput"),
                   wgt=dt("dbg_wgt", [P, 128], F32, "ExternalOutput"), acc=dt("dbg_acc", [P, D], F32, "ExternalOutput"),
                   s16=dt("dbg_s16", [P, 256], F32, "ExternalOutput"), i16=dt("dbg_i16", [P, 256], F32, "ExternalOutput"),
                   tops=dt("dbg_tops", [P, 128], F32, "ExternalOutput"), pos=dt("dbg_pos", [P, 128], F32, "ExternalOutput"),
                   h=dt("dbg_h", [P, D], F32, "ExternalOutput"))

    with ExitStack() as st:
        pg = Prog(nc, st)
        V = lambda fn, r=(), w=(): pg.op(pg.dve, fn, r, w)
        A = lambda fn, r=(), w=(): pg.op(pg.act, fn, r, w)
        G = lambda fn, r=(), w=(): pg.op(pg.pool, fn, r, w)
        T = lambda fn, r=(), w=(): pg.op(pg.pe, fn, r, w)
        LD = lambda dst, dst_ap, src_ap: pg.dma(pg.sp, lambda e: e.dma_start(out=dst_ap, in_=src_ap), dst=dst)

        H = [pg.sbuf(f"H{i}", [P, D], F32) for i in range(NT)]
        ID = pg.sbuf("ID", [P, P], F32)
        IDb = pg.sbuf("IDb", [P, P], BF16)
        COS = pg.sbuf("COS", [P, NT, 32], F32)
        SIN = pg.sbuf("SIN", [P, NT, 32], F32)
        NEGD = pg.sbuf("NEGD", [P, 384], F32)
        IOTA = pg.sbuf("IOTA", [P, 16], F32)
        Gt = pg.sbuf("Gt", [P, D], F32)
        Bt = pg.sbuf("Bt", [P, D], F32)
        STt = pg.sbuf("STt", [P, 2, 6], F32)
        MV = pg.sbuf("MV", [P, 2], F32)
        RS = pg.sbuf("RS", [P, 1], F32)
        STG = [pg.sbuf(f"STG{i}", [P, 512], F32) for i in range(2)]
        HTt = pg.sbuf("HTt", [P, 8, P], BF16)
        R = pg.sbuf("R", [P, D], F32)
        PS0 = pg.psum("PS0", [P, 512]); PS1 = pg.psum("PS1", [P, 512])
        PS23 = pg.psum("PS23", [P, 1024]); PS45 = pg.psum("PS45", [P, 1024])
        PS6 = pg.psum("PS6", [P, 512]); PS7 = PS6
        PSB = pg.psum("PSB", [P, 1024], BF16)

        LD(ID, ID[:], c_ident)
        V(lambda e: e.tensor_copy(out=IDb[:], in_=ID[:]), [ID], [IDb])
        LD(COS, COS[:], c_cos); LD(SIN, SIN[:], c_sin); LD(NEGD, NEGD[:], c_negd); LD(IOTA, IOTA[:], c_iota)

        stg_i = [0]

        def load_cast(dst, dst_ap, src_ap, ncols):
            sb = STG[stg_i[0] % 2]; k = stg_i[0]; stg_i[0] += 1
            LD(sb, sb[:, 0:ncols], src_ap)
            if k % 2 == 0:
                V(lambda e: e.tensor_copy(out=dst_ap, in_=sb[:, 0:ncols]), [sb], [dst])
            else:
                A(lambda e: e.copy(out=dst_ap, in_=sb[:, 0:ncols]), [sb], [dst])

        def layer_norm(src_ap, src_bufs, dst_ap, dst_buf):
            for c in range(2):
                V(lambda e, c=c: e.bn_stats(out=STt[:, c, :], in_=src_ap[:, c * 512:(c + 1) * 512]), src_bufs, [STt])
            V(lambda e: e.bn_aggr(out=MV[:], in_=STt[:]), [STt], [MV])
            V(lambda e: e.tensor_scalar(out=RS[:], in0=MV[:, 1:2], scalar1=LN_EPS, scalar2=None, op0=ALU.add), [MV], [RS])
            A(lambda e: e.sqrt(out=RS[:], in_=RS[:]), [RS], [RS])
            V(lambda e: e.reciprocal(out=RS[:], in_=RS[:]), [RS], [RS])
            V(lambda e: e.tensor_scalar(out=dst_ap, in0=src_ap, scalar1=MV[:, 0:1], scalar2=RS[:, 0:1],
                                        op0=ALU.subtract, op1=ALU.mult), list(src_bufs) + [MV, RS], [dst_buf])
            G(lambda e: e.tensor_tensor(out=dst_ap, in0=dst_ap, in1=Gt[:], op=ALU.mult), [dst_buf, Gt], [dst_buf])
            G(lambda e: e.tensor_tensor(out=dst_ap, in0=dst_ap, in1=Bt[:], op=ALU.add), [dst_buf, Bt], [dst_buf])

        def load_ln_params(g_ap, b_ap):
            LD(Gt, Gt[:], g_ap.partition_broadcast(P))
            LD(Bt, Bt[:], b_ap.partition_broadcast(P))

        def make_hT(hb):
            for c in range(8):
                ps = PS0 if c < 4 else PS1
                T(lambda e, c=c, ps=ps: e.transpose(out=ps[:, (c % 4) * P:(c % 4 + 1) * P], in_=hb[:, c * P:(c + 1) * P], identity=ID[:]),
                  [hb, ID], [ps])
            V(lambda e: e.tensor_copy(out=HTt[:, 0:4, :], in_=PS0[:].rearrange("p (c f) -> p c f", c=4)), [PS0], [HTt])
            A(lambda e: e.copy(out=HTt[:, 4:8, :], in_=PS1[:].rearrange("p (c f) -> p c f", c=4)), [PS1], [HTt])

        def rms_rope(ph, src, src_buf, nh, gain, ti, dst_view, dst_buf, sc):
            SQ, SSq, XN, TA, TB = sc["SQ"], sc["SSq"], sc["XN"], sc["TA"], sc["TB"]
            V(lambda e: e.tensor_tensor(out=SQ[:, 0:nh, :], in0=src, in1=src, op=ALU.mult), [src_buf], [SQ])
            V(lambda e: e.tensor_reduce(out=SSq[:, 0:nh], in_=SQ[:, 0:nh, :], axis=AX.X, op=ALU.add), [SQ], [SSq])
            V(lambda e: e.tensor_scalar(out=SSq[:, 0:nh], in0=SSq[:, 0:nh], scalar1=1.0 / 64, scalar2=RMS_EPS, op0=ALU.mult, op1=ALU.add), [SSq], [SSq])
            A(lambda e: e.sqrt(out=SSq[:, 0:nh], in_=SSq[:, 0:nh]), [SSq], [SSq])
            V(lambda e: e.reciprocal(out=SSq[:, 0:nh], in_=SSq[:, 0:nh]), [SSq], [SSq])
            V(lambda e: e.tensor_tensor(out=XN[:, 0:nh, :], in0=src, in1=SSq[:, 0:nh].unsqueeze(2).to_broadcast([P, nh, 64]), op=ALU.mult), [src_buf, SSq], [XN])
            G(lambda e: e.tensor_tensor(out=XN[:, 0:nh, :], in0=XN[:, 0:nh, :], in1=gain[:, :].unsqueeze(1).to_broadcast([P, nh, 64]), op=ALU.mult), [XN, gain], [XN])
            xv = XN[:, 0:nh, :].rearrange("p h (a b f) -> p h a b f", a=2, b=2)
            x1 = xv[:, :, :, 0, :]; x2 = xv[:, :, :, 1, :]
            cb = COS[:, ti, :].rearrange("p (a f) -> p a f", a=2).unsqueeze(1).to_broadcast([P, nh, 2, 16])
            sb_ = SIN[:, ti, :].rearrange("p (a f) -> p a f", a=2).unsqueeze(1).to_broadcast([P, nh, 2, 16])
            dv = dst_view.rearrange("p h (a b f) -> p h a b f", a=2, b=2)
            ta = TA[:, 0:nh, :].rearrange("p h (a f) -> p h a f", a=2)
            tb = TB[:, 0:nh, :].rearrange("p h (a f) -> p h a f", a=2)
            V(lambda e: e.tensor_tensor(out=ta, in0=x1, in1=cb, op=ALU.mult), [XN, COS], [TA])
            G(lambda e: e.tensor_tensor(out=tb, in0=x2, in1=sb_, op=ALU.mult), [XN, SIN], [TB])
            V(lambda e: e.tensor_tensor(out=dv[:, :, :, 0, :], in0=ta, in1=tb, op=ALU.subtract), [TA, TB], [dst_buf])
            V(lambda e: e.tensor_tensor(out=ta, in0=x1, in1=sb_, op=ALU.mult), [XN, SIN], [TA])
            G(lambda e: e.tensor_tensor(out=tb, in0=x2, in1=cb, op=ALU.mult), [XN, COS], [TB])
            V(lambda e: e.tensor_tensor(out=dv[:, :, :, 1, :], in0=ta, in1=tb, op=ALU.add), [TA, TB], [dst_buf])

        def attention_layer(l):
            with ExitStack() as ph:
                KT_A = pg.sbuf("KT_A", [P, 2, S], BF16, ph)
                KT_B = pg.sbuf("KT_B", [P, 2, S], BF16, ph)
                VA = pg.sbuf("VA", [P, NT, 2, 65], BF16, ph)
                VB = pg.sbuf("VB", [P, NT, 2, 64], BF16, ph)
                GQ = pg.sbuf("GQ", [P, 64], F32, ph)
                GK = pg.sbuf("GK", [P, 64], F32, ph)
                SINK = pg.sbuf("SINK", [P, 8], F32, ph)
                GOT = pg.sbuf("GOT", [P, 8], F32, ph)
                sc = dict(SQ=pg.sbuf("SQ", [P, 8, 64], F32, ph), SSq=pg.sbuf("SSq", [P, 8], F32, ph),
                          XN=pg.sbuf("XN", [P, 8, 64], F32, ph), TA=pg.sbuf("TA", [P, 8, 32], F32, ph),
                          TB=pg.sbuf("TB", [P, 8, 32], F32, ph))
                LD(GQ, GQ[:], qn_g[l:l + 1, :].partition_broadcast(P))
                V(lambda e: e.tensor_scalar(out=GQ[:], in0=GQ[:], scalar1=0.125, scalar2=None, op0=ALU.mult), [GQ], [GQ])
                LD(GK, GK[:], kn_g[l:l + 1, :].partition_broadcast(P))
                LD(SINK, SINK[:], sink[l:l + 1, :].partition_broadcast(P))
                LD(GOT, GOT[:], gnT[l])
                V(lambda e: e.memset(VA[:], 1.0), [], [VA])
                with ExitStack() as sp:
                    WKV = pg.sbuf("WKV", [P, 8, 512], BF16, sp)
                    KV32 = pg.sbuf("KV32", [P, 512], F32, sp)
                    KAb = pg.sbuf("KAb", [P, 2, 2, 64], BF16, sp)
                    KBb = pg.sbuf("KBb", [P, 2, 2, 64], BF16, sp)
                    for kc in range(8):
                        rows = slice(kc * P, (kc + 1) * P)
                        load_cast(WKV, WKV[:, kc, 0:256], w_in[l, rows, 512:768], 256)
                        load_cast(WKV, WKV[:, kc, 256:512], w_in[l, rows, 1280:1536], 256)
                    for ti in range(NT):
                        make_hT(H[ti])
                        for kc in range(8):
                            T(lambda e, kc=kc: e.matmul(PS6[:], lhsT=HTt[:, kc, :], rhs=WKV[:, kc, :], start=(kc == 0), stop=(kc == 7)),
                              [HTt, WKV], [PS6])
                        A(lambda e: e.copy(out=KV32[:], in_=PS6[:]), [PS6], [KV32])
                        ka = KV32[:, 0:128].rearrange("p (h d) -> p h d", h=2)
                        rms_rope(sp, ka, KV32, 2, GK, ti, KAb[:, :, 0, :], KAb, sc)
                        V(lambda e: e.tensor_copy(out=KAb[:, :, 1, :], in_=KAb[:, :, 0, :]), [KAb], [KAb])
                        kb = KV32[:, 256:384].rearrange("p (h d) -> p h d", h=2)
                        for dup in range(2):
                            V(lambda e, dup=dup: e.tensor_copy(out=KBb[:, :, dup, :], in_=kb), [KV32], [KBb])
                        A(lambda e, ti=ti: e.copy(out=VA[:, ti, :, 0:64], in_=KV32[:, 128:256].rearrange("p (h d) -> p h d", h=2)), [KV32], [VA])
                        A(lambda e, ti=ti: e.copy(out=VB[:, ti, :, :], in_=KV32[:, 384:512].rearrange("p (h d) -> p h d", h=2)), [KV32], [VB])
                        for kv in range(2):
                            T(lambda e, kv=kv: e.transpose(out=PSB[:, kv * P:(kv + 1) * P], in_=KAb[:, kv, :, :].rearrange("p a d -> p (a d)"), identity=IDb[:]),
                              [KAb, IDb], [PSB])
                            T(lambda e, kv=kv: e.transpose(out=PSB[:, (2 + kv) * P:(3 + kv) * P], in_=KBb[:, kv, :, :].rearrange("p a d -> p (a d)"), identity=IDb[:]),
                              [KBb, IDb], [PSB])
                        V(lambda e, ti=ti: e.tensor_copy(out=KT_A[:, :, ti * P:(ti + 1) * P], in_=PSB[:, 0:256].rearrange("p (k t) -> p k t", k=2)), [PSB], [KT_A])
                        V(lambda e, ti=ti: e.tensor_copy(out=KT_B[:, :, ti * P:(ti + 1) * P], in_=PSB[:, 256:512].rearrange("p (k t) -> p k t", k=2)), [PSB], [KT_B])
                pg.barrier()
                with ExitStack() as sg:
                    WQ = pg.sbuf("WQ", [P, 8, 1024], BF16, sg)
                    WO = pg.sbuf("WO", [P, 8, 1024], BF16, sg)
                    QA32 = pg.sbuf("QA32", [P, 512], F32, sg)
                    QAb = pg.sbuf("QAb", [P, 512], BF16, sg)
                    QBb = pg.sbuf("QBb", [P, 512], BF16, sg)
                    QTt = pg.sbuf("QTt", [P, 8, P], BF16, sg)
                    PTb = [pg.sbuf(f"PTb{i}", [P, 1024], BF16, sg) for i in range(2)]
                    OALL = pg.sbuf("OALL", [P, 8, 65], F32, sg)
                    RZ = pg.sbuf("RZ", [P, 8], F32, sg)
                    OT = pg.sbuf("OT", [P, 16, 64], F32, sg)
                    OSQ = pg.sbuf("OSQ", [P, 16, 64], F32, sg)
                    OSS = pg.sbuf("OSS", [P, 16], F32, sg)
                    OTb = pg.sbuf("OTb", [P, 1024], BF16, sg)
                    OTt = pg.sbuf("OTt", [P, 8, P], BF16, sg)
                    SB = pg.sbuf("SB", [P, 8, 384], F32, sg)
                    PB = pg.sbuf("PB", [P, 8, 384], BF16, sg)
                    PTB = [pg.sbuf(f"PTB{i}", [P, 384], BF16, sg) for i in range(2)]
                    MX = pg.sbuf("MX", [P, 8], F32, sg)
                    NM = pg.sbuf("NM", [P, 8], F32, sg)
                    ES = pg.sbuf("ES", [P, 8], F32, sg)
                    RSUM = pg.sbuf("RSUM", [P, 8], F32, sg)
                    ZB = pg.sbuf("ZB", [P, 8], F32, sg)
                    for kc in range(8):
                        rows = slice(kc * P, (kc + 1) * P)
                        load_cast(WQ, WQ[:, kc, 0:512], w_in[l, rows, 0:512], 512)
                        load_cast(WQ, WQ[:, kc, 512:1024], w_in[l, rows, 768:1280], 512)
                        load_cast(WO, WO[:, kc, 0:512], w_o[l, rows, 0:512], 512)
                        load_cast(WO, WO[:, kc, 512:1024], w_o[l, rows, 512:1024], 512)
                    load_ln_params(ln1_g[l:l + 1, :], ln1_b[l:l + 1, :])
                    for qt in range(NT):
                        make_hT(H[qt])
                        for nb in range(2):
                            for kc in range(8):
                                T(lambda e, nb=nb, kc=kc: e.matmul(PS23[:, nb * 512:(nb + 1) * 512], lhsT=HTt[:, kc, :], rhs=WQ[:, kc, nb * 512:(nb + 1) * 512],
                                                                   start=(kc == 0), stop=(kc == 7)), [HTt, WQ], [PS23])
                        A(lambda e: e.copy(out=QA32[:], in_=PS23[:, 0:512]), [PS23], [QA32])
                        A(lambda e: e.mul(out=QBb[:], in_=PS23[:, 512:1024], mul=0.125), [PS23], [QBb])
                        rms_rope(sg, QA32[:].rearrange("p (h d) -> p h d", h=8), QA32, 8, GQ, qt,
                                 QAb[:].rearrange("p (h d) -> p h d", h=8), QAb, sc)
                        for hp in range(4):
                            T(lambda e, hp=hp: e.transpose(out=PSB[:, hp * P:(hp + 1) * P], in_=QAb[:, hp * P:(hp + 1) * P], identity=IDb[:]), [QAb, IDb], [PSB])
                            T(lambda e, hp=hp: e.transpose(out=PSB[:, (4 + hp) * P:(5 + hp) * P], in_=QBb[:, hp * P:(hp + 1) * P], identity=IDb[:]), [QBb, IDb], [PSB])
                        V(lambda e: e.tensor_copy(out=QTt[:], in_=PSB[:].rearrange("p (c t) -> p c t", c=8)), [PSB], [QTt])
                        blocks = [(h, hs) for h in range(8) for hs in range(2)]
                        SPS = [PS23, PS45]
                        OPS = [PS0, PS1]

                        def emit_S(b):
                            h, hs = blocks[b]
                            kv, half, hp = h // 4, h % 2, h // 2
                            sps = SPS[b % 2]
                            pr = slice(half * 64, (half + 1) * 64)
                            for j in range(8):
                                stt = hs * 8 + j
                                T(lambda e, j=j, stt=stt: e.matmul(sps[:, j * P:(j + 1) * P], lhsT=KT_A[pr, kv, stt * P:(stt + 1) * P], rhs=QTt[pr, hp, :],
                                                                     start=True, stop=True), [KT_A, QTt], [sps])
                            A(lambda e: e.activation(out=PTb[b % 2][:], in_=sps[:], func=AF.Exp), [sps], [PTb[b % 2]])

                        def emit_PV(b):
                            h, hs = blocks[b]
                            kv = h // 4
                            ops = OPS[h // 4]
                            g = h % 4
                            for j in range(8):
                                stt = hs * 8 + j
                                T(lambda e, j=j, stt=stt: e.matmul(ops[:, g * 65:(g + 1) * 65], lhsT=PTb[b % 2][:, j * P:(j + 1) * P], rhs=VA[:, stt, kv, :],
                                                                     start=(stt == 0), stop=(stt == 15)), [PTb[b % 2], VA], [ops])
                            if hs == 1 and g == 3:
                                A(lambda e: e.copy(out=OALL[:, (h // 4) * 4:(h // 4) * 4 + 4, :], in_=ops[:, 0:260].rearrange("p (g c) -> p g c", g=4)), [ops], [OALL])

                        emit_S(0)
                        for b in range(len(blocks)):
                            if b + 1 < len(blocks):
                                emit_S(b + 1)
                            emit_PV(b)
                        V(lambda e: e.reciprocal(out=RZ[:], in_=OALL[:, :, 64]), [OALL], [RZ])
                        V(lambda e: e.tensor_tensor(out=OT[:, 0:8, :], in0=OALL[:, :, 0:64], in1=RZ[:, :].unsqueeze(2).to_broadcast([P, 8, 64]), op=ALU.mult), [OALL, RZ], [OT])
                        lo = max(qt - 1, 0); hi = min(qt + 1, NT - 1); nk = hi - lo + 1; ncol = nk * P
                        c0 = (lo - (qt - 1)) * P
                        for h in range(8):
                            kv, half, hp = h // 4, h % 2, h // 2
                            pr = slice(half * 64, (half + 1) * 64)
                            psb = PS6 if h % 2 == 0 else PS7
                            T(lambda e, psb=psb, kv=kv, hp=hp, pr=pr: e.matmul(psb[:, 0:ncol], lhsT=QTt[pr, 4 + hp, :], rhs=KT_B[pr, kv, lo * P:(hi + 1) * P], start=True, stop=True),
                              [QTt, KT_B], [psb])
                            slope = float(2.0 ** (-(h + 1)))
                            V(lambda e, psb=psb, h=h, slope=slope: e.scalar_tensor_tensor(out=SB[:, h, 0:ncol], in0=NEGD[:, c0:c0 + ncol], scalar=slope, in1=psb[:, 0:ncol],
                                                                                           op0=ALU.mult, op1=ALU.add), [NEGD, psb], [SB])
                        V(lambda e: e.tensor_reduce(out=MX[:], in_=SB[:, :, 0:ncol], axis=AX.X, op=ALU.max), [SB], [MX])
                        V(lambda e: e.tensor_tensor(out=MX[:], in0=MX[:], in1=SINK[:], op=ALU.max), [MX, SINK], [MX])
                        V(lambda e: e.tensor_scalar(out=NM[:], in0=MX[:], scalar1=-1.0, scalar2=None, op0=ALU.mult), [MX], [NM])
                        V(lambda e: e.tensor_tensor(out=ES[:], in0=SINK[:], in1=MX[:], op=ALU.subtract), [SINK, MX], [ES])
                        A(lambda e: e.activation(out=ES[:], in_=ES[:], func=AF.Exp), [ES], [ES])
                        for h in range(8):
                            A(lambda e, h=h: e.activation(out=PB[:, h, 0:ncol], in_=SB[:, h, 0:ncol], func=AF.Exp, bias=NM[:, h:h + 1], scale=1.0,
                                                          accum_out=RSUM[:, h:h + 1]), [SB, NM], [PB, RSUM])
                        V(lambda e: e.tensor_tensor(out=ZB[:], in0=RSUM[:], in1=ES[:], op=ALU.add), [RSUM, ES], [ZB])
                        V(lambda e: e.reciprocal(out=ZB[:], in_=ZB[:]), [ZB], [ZB])
                        for h in range(8):
                            kv = h // 4
                            ptb = PTB[h % 2]
                            for j in range(nk):
                                T(lambda e, h=h, j=j: e.transpose(out=PSB[:, j * P:(j + 1) * P], in_=PB[:, h, j * P:(j + 1) * P], identity=IDb[:]), [PB, IDb], [PSB])
                            V(lambda e, ptb=ptb: e.tensor_copy(out=ptb[:, 0:ncol], in_=PSB[:, 0:ncol]), [PSB], [ptb])
                            for j in range(nk):
                                T(lambda e, h=h, j=j, ptb=ptb, kv=kv: e.matmul(PS1[:, h * 64:(h + 1) * 64], lhsT=ptb[:, j * P:(j + 1) * P], rhs=VB[:, lo + j, kv, :],
                                                                                 start=(j == 0), stop=(j == nk - 1)), [ptb, VB], [PS1])
                        V(lambda e: e.tensor_tensor(out=OT[:, 8:16, :], in0=PS1[:].rearrange("p (h d) -> p h d", h=8), in1=ZB[:, :].unsqueeze(2).to_broadcast([P, 8, 64]), op=ALU.mult),
                          [PS1, ZB], [OT])
                        V(lambda e: e.tensor_tensor(out=OSQ[:], in0=OT[:], in1=OT[:], op=ALU.mult), [OT], [OSQ])
                        V(lambda e: e.tensor_reduce(out=OSS[:], in_=OSQ[:], axis=AX.X, op=ALU.add), [OSQ], [OSS])
                        V(lambda e: e.tensor_scalar(out=OSS[:], in0=OSS[:], scalar1=1.0 / 64, scalar2=RMS_EPS, op0=ALU.mult, op1=ALU.add), [OSS], [OSS])
                        A(lambda e: e.sqrt(out=OSS[:], in_=OSS[:]), [OSS], [OSS])
                        V(lambda e: e.reciprocal(out=OSS[:], in_=OSS[:]), [OSS], [OSS])
                        V(lambda e: e.tensor_tensor(out=OTb[:].rearrange("p (h d) -> p h d", h=16), in0=OT[:], in1=OSS[:, :].unsqueeze(2).to_broadcast([P, 16, 64]), op=ALU.mult),
                          [OT, OSS], [OTb])
                        for kc in range(8):
                            T(lambda e, kc=kc: e.transpose(out=PSB[:, kc * P:(kc + 1) * P], in_=OTb[:, kc * P:(kc + 1) * P], identity=IDb[:]), [OTb, IDb], [PSB])
                        V(lambda e: e.tensor_tensor(out=OTt[:], in0=PSB[:].rearrange("p (c t) -> p c t", c=8), in1=GOT[:, :].unsqueeze(2).to_broadcast([P, 8, P]), op=ALU.mult),
                          [PSB, GOT], [OTt])
                        for nb in range(2):
                            for kc in range(8):
                                T(lambda e, nb=nb, kc=kc: e.matmul(PS45[:, nb * 512:(nb + 1) * 512], lhsT=OTt[:, kc, :], rhs=WO[:, kc, nb * 512:(nb + 1) * 512],
                                                                   start=(kc == 0), stop=(kc == 7)), [OTt, WO], [PS45])
                        V(lambda e, qt=qt: e.scalar_tensor_tensor(out=R[:], in0=H[qt][:], scalar=ALPHA, in1=PS45[:], op0=ALU.mult, op1=ALU.add), [H[qt], PS45], [R])
                        layer_norm(R[:], [R], H[qt][:], H[qt])
                pg.barrier()

        def peer_layer(l, seq, last):
            NS = peer_slots
            with ExitStack() as ph:
                EIDX = pg.sbuf("EIDX", [P, NT, 128], I32, ph)
                GW = pg.sbuf("GW", [P, NT, 128], F32, ph)
                with ExitStack() as s1:
                    WQp = pg.sbuf("WQp", [P, 8, 2048], BF16, s1)
                    KEYST = pg.sbuf("KEYST", [P, 16, P], BF16, s1)
                    K32 = pg.sbuf("K32", [P, P], F32, s1)
                    QTp = pg.sbuf("QTp", [P, 16, P], BF16, s1)
                    SC = pg.sbuf("SC", [P, 16, P], F32, s1)
                    SC2 = pg.sbuf("SC2", [P, 16, P], F32, s1)
                    S16 = pg.sbuf("S16", [P, 16, 16], F32, s1)
                    I16u = pg.sbuf("I16u", [P, 16, 16], U32, s1)
                    I16f = pg.sbuf("I16f", [P, 16, 16], F32, s1)
                    CAND = pg.sbuf("CAND", [P, 8, 256], F32, s1)
                    CAND2 = pg.sbuf("CAND2", [P, 8, 256], F32, s1)
                    TOPS = pg.sbuf("TOPS", [P, 8, 16], F32, s1)
                    POSu = pg.sbuf("POSu", [P, 8, 16], U32, s1)
                    POSf = pg.sbuf("POSf", [P, 8, 16], F32, s1)
                    AFl = pg.sbuf("AFl", [P, 8, 16], F32, s1)
                    BFl = pg.sbuf("BFl", [P, 8, 16], F32, s1)
                    OH = pg.sbuf("OH", [P, 8, 16, 16], F32, s1)
                    I1S = pg.sbuf("I1S", [P, 8, 16], F32, s1)
                    I2S = pg.sbuf("I2S", [P, 8, 16], F32, s1)
                    EF = pg.sbuf("EF", [P, 8, 16], F32, s1)
                    MXT = pg.sbuf("MXT", [P, 8], F32, s1)
                    TS = pg.sbuf("TS", [P, 8, 16], F32, s1)
                    SM = pg.sbuf("SM", [P, 8], F32, s1)
                    for kc in range(8):
                        rows = slice(kc * P, (kc + 1) * P)
                        for cc in range(4):
                            load_cast(WQp, WQp[:, kc, cc * 512:(cc + 1) * 512], peer_wq[l, rows, cc * 512:(cc + 1) * 512], 512)
                    for hp in range(16):
                        LD(K32, K32[:], peer_keys[l, hp])
                        T(lambda e: e.transpose(out=PS6[:, 0:P], in_=K32[:], identity=ID[:]), [K32, ID], [PS6])
                        V(lambda e, hp=hp: e.tensor_copy(out=KEYST[:, hp, :], in_=PS6[:, 0:P]), [PS6], [KEYST])
                    iota4 = IOTA[:, :].unsqueeze(1).unsqueeze(1).to_broadcast([P, 8, 16, 16])
                    for ti in range(NT):
                        make_hT(H[ti])
                        for hp in range(16):
                            ps = PS23 if (hp // 4) % 2 == 0 else PS45
                            for kc in range(8):
                                T(lambda e, hp=hp, kc=kc, ps=ps: e.matmul(ps[:, (hp % 4) * P:(hp % 4 + 1) * P], lhsT=WQp[:, kc, hp * P:(hp + 1) * P], rhs=HTt[:, kc, :],
                                                                          start=(kc == 0), stop=(kc == 7)), [WQp, HTt], [ps])
                            if hp % 4 == 3:
                                g4 = hp // 4
                                if g4 % 2 == 0:
                                    V(lambda e, g4=g4, ps=ps: e.tensor_copy(out=QTp[:, g4 * 4:g4 * 4 + 4, :], in_=ps[:, 0:512].rearrange("p (c t) -> p c t", c=4)), [ps], [QTp])
                                else:
                                    A(lambda e, g4=g4, ps=ps: e.copy(out=QTp[:, g4 * 4:g4 * 4 + 4, :], in_=ps[:, 0:512].rearrange("p (c t) -> p c t", c=4)), [ps], [QTp])
                        for hp in range(16):
                            ps = PS6 if (hp // 4) % 2 == 0 else PS0
                            T(lambda e, hp=hp, ps=ps: e.matmul(ps[:, (hp % 4) * P:(hp % 4 + 1) * P], lhsT=QTp[:, hp, :], rhs=KEYST[:, hp, :], start=True, stop=True),
                              [QTp, KEYST], [ps])
                            if hp % 4 == 3:
                                g4 = hp // 4
                                A(lambda e, g4=g4, ps=ps: e.copy(out=SC[:, g4 * 4:g4 * 4 + 4, :], in_=ps[:].rearrange("p (c t) -> p c t", c=4)), [ps], [SC])

                        def top16(vals, vals2, vbufs, outv, outi, obufs):
                            V(lambda e: e.max(out=outv[:, 0:8], in_=vals), vbufs[:1], [obufs[0]])
                            V(lambda e: e.max_index(out=outi[:, 0:8], in_max=outv[:, 0:8], in_values=vals), [vbufs[0], obufs[0]], [obufs[1]])
                            V(lambda e: e.match_replace(out=vals2, in_to_replace=outv[:, 0:8], in_values=vals, imm_value=-1e30), [vbufs[0], obufs[0]], [vbufs[1]])
                            V(lambda e: e.max(out=outv[:, 8:16], in_=vals2), [vbufs[1]], [obufs[0]])
                            V(lambda e: e.max_index(out=outi[:, 8:16], in_max=outv[:, 8:16], in_values=vals2), [vbufs[1], obufs[0]], [obufs[1]])

                        for hp in range(16):
                            top16(SC[:, hp, :], SC2[:, hp, :], [SC, SC2], S16[:, hp, :], I16u[:, hp, :], [S16, I16u])
                        s16v = S16[:].rearrange("p (h q) k -> p h q k", q=2)
                        V(lambda e: e.tensor_tensor(out=CAND[:].rearrange("p h (a b) -> p h a b", a=16),
                                                    in0=s16v[:, :, 0, :].unsqueeze(3).to_broadcast([P, 8, 16, 16]),
                                                    in1=s16v[:, :, 1, :].unsqueeze(2).to_broadcast([P, 8, 16, 16]), op=ALU.add), [S16], [CAND])
                        for h in range(8):
                            top16(CAND[:, h, :], CAND2[:, h, :], [CAND, CAND2], TOPS[:, h, :], POSu[:, h, :], [TOPS, POSu])
                        V(lambda e: e.tensor_copy(out=I16f[:], in_=I16u[:]), [I16u], [I16f])
                        V(lambda e: e.tensor_copy(out=POSf[:], in_=POSu[:]), [POSu], [POSf])
                        V(lambda e: e.tensor_scalar(out=AFl[:], in0=POSf[:], scalar1=0.0625, scalar2=-0.46875, op0=ALU.mult, op1=ALU.add), [POSf], [AFl])
                        V(lambda e: e.tensor_scalar(out=AFl[:], in0=AFl[:], scalar1=12582912.0, scalar2=-12582912.0, op0=ALU.add, op1=ALU.add), [AFl], [AFl])
                        V(lambda e: e.scalar_tensor_tensor(out=BFl[:], in0=AFl[:], scalar=-16.0, in1=POSf[:], op0=ALU.mult, op1=ALU.add), [AFl, POSf], [BFl])
                        i16v = I16f[:].rearrange("p (h q) k -> p h q k", q=2)
                        for (sel, qq, dst) in ((AFl, 0, I1S), (BFl, 1, I2S)):
                            V(lambda e, sel=sel: e.tensor_tensor(out=OH[:], in0=iota4, in1=sel[:, :, :].unsqueeze(3).to_broadcast([P, 8, 16, 16]), op=ALU.is_equal), [IOTA, sel], [OH])
                            V(lambda e, qq=qq: e.tensor_tensor(out=OH[:], in0=OH[:], in1=i16v[:, :, qq, :].unsqueeze(2).to_broadcast([P, 8, 16, 16]), op=ALU.mult), [OH, I16f], [OH])
                            V(lambda e, dst=dst: e.tensor_reduce(out=dst[:], in_=OH[:], axis=AX.X, op=ALU.add), [OH], [dst])
                        V(lambda e: e.scalar_tensor_tensor(out=EF[:], in0=I1S[:], scalar=128.0, in1=I2S[:], op0=ALU.mult, op1=ALU.add), [I1S, I2S], [EF])
                        V(lambda e, ti=ti: e.tensor_copy(out=EIDX[:, ti, :].rearrange("p (h k) -> p h k", h=8), in_=EF[:]), [EF], [EIDX])
                        V(lambda e: e.tensor_reduce(out=MXT[:], in_=TOPS[:], axis=AX.X, op=ALU.max), [TOPS], [MXT])
                        V(lambda e: e.tensor_tensor(out=TS[:], in0=TOPS[:], in1=MXT[:, :].unsqueeze(2).to_broadcast([P, 8, 16]), op=ALU.subtract), [TOPS, MXT], [TS])
                        A(lambda e: e.activation(out=TS[:], in_=TS[:], func=AF.Exp), [TS], [TS])
                        V(lambda e: e.tensor_reduce(out=SM[:], in_=TS[:], axis=AX.X, op=ALU.add), [TS], [SM])
                        V(lambda e: e.reciprocal(out=SM[:], in_=SM[:]), [SM], [SM])
                        V(lambda e, ti=ti: e.tensor_tensor(out=GW[:, ti, :].rearrange("p (h k) -> p h k", h=8), in0=TS[:], in1=SM[:, :].unsqueeze(2).to_broadcast([P, 8, 16]), op=ALU.mult),
                          [TS, SM], [GW])
                        if debug and ti == 0 and l == 0:
                            pg.dma(pg.sp, lambda e: e.dma_start(out=dbg["sc"], in_=SC[:].rearrange("p a b -> p (a b)")), src=SC)
                            pg.dma(pg.sp, lambda e: e.dma_start(out=dbg["s16"], in_=S16[:].rearrange("p a b -> p (a b)")), src=S16)
                            pg.dma(pg.sp, lambda e: e.dma_start(out=dbg["i16"], in_=I16f[:].rearrange("p a b -> p (a b)")), src=I16f)
                            pg.dma(pg.sp, lambda e: e.dma_start(out=dbg["tops"], in_=TOPS[:].rearrange("p a b -> p (a b)")), src=TOPS)
                            pg.dma(pg.sp, lambda e: e.dma_start(out=dbg["pos"], in_=POSf[:].rearrange("p a b -> p (a b)")), src=POSf)
                            pg.dma(pg.sp, lambda e: e.dma_start(out=dbg["eidx"], in_=EIDX[:, 0, :]), src=EIDX)
                            pg.dma(pg.sp, lambda e: e.dma_start(out=dbg["gw"], in_=GW[:, 0, :]), src=GW)
                            pg.dma(pg.sp, lambda e: e.dma_start(out=dbg["h"], in_=H[0][:]), src=H[0])
                pg.barrier()
                with ExitStack() as s2:
                    GS = 2
                    UB = [pg.sbuf(f"UB{i}", [P, GS, D], F32, s2) for i in range(2)]
                    VBf = [pg.sbuf(f"VBf{i}", [P, GS, D], F32, s2) for i in range(2)]
                    JUNK = pg.sbuf("JUNK", [P, D], F32, s2)
                    ADOT = pg.sbuf("ADOT", [P, 128], F32, s2)
                    WGT = pg.sbuf("WGT", [P, 128], F32, s2)
                    ACC = pg.sbuf("ACC", [P, D], F32, s2)
                    load_ln_params(ln2_g[l:l + 1, :], ln2_b[l:l + 1, :])
                    ng = NS // GS
                    for ti in range(NT):
                        if NS < 128:
                            V(lambda e: e.memset(ADOT[:], 0.0), [], [ADOT])
                        for tab, bufs, which in ((peer_u, UB, 0), (peer_v, VBf, 1)):
                            if which == 1:
                                A(lambda e: e.activation(out=WGT[:], in_=ADOT[:], func=AF.Gelu), [ADOT], [WGT])
                                V(lambda e, ti=ti: e.tensor_tensor(out=WGT[:], in0=WGT[:], in1=GW[:, ti, :], op=ALU.mult), [WGT, GW], [WGT])
                            for grp in range(ng):
                                bb = bufs[grp % 2]
                                for jj in range(GS):
                                    j = grp * GS + jj
                                    pg.dma(pg.pool, lambda e, bb=bb, jj=jj, j=j, ti=ti, tab=tab: e.indirect_dma_start(
                                        out=bb[:, jj, :], out_offset=None, in_=tab[l],
                                        in_offset=bass.IndirectOffsetOnAxis(ap=EIDX[:, ti, j:j + 1], axis=0)), dst=bb, extra_reads=[EIDX])
                                for jj in range(GS):
                                    j = grp * GS + jj
                                    if which == 0:
                                        V(lambda e, bb=bb, jj=jj, j=j, ti=ti: e.scalar_tensor_tensor(out=JUNK[:], in0=bb[:, jj, :], scalar=1.0, in1=H[ti][:], op0=ALU.mult, op1=ALU.mult,
                                                                                                     accum_out=ADOT[:, j:j + 1]), [bb, H[ti]], [JUNK, ADOT])
                                    elif j == 0:
                                        V(lambda e, bb=bb, jj=jj, j=j: e.tensor_scalar(out=ACC[:], in0=bb[:, jj, :], scalar1=WGT[:, j:j + 1], scalar2=None, op0=ALU.mult), [bb, WGT], [ACC])
                                    else:
                                        V(lambda e, bb=bb, jj=jj, j=j: e.scalar_tensor_tensor(out=ACC[:], in0=bb[:, jj, :], scalar=WGT[:, j:j + 1], in1=ACC[:], op0=ALU.mult, op1=ALU.add),
                                          [bb, WGT, ACC], [ACC])
                        if debug and ti == 0 and l == 0:
                            pg.dma(pg.sp, lambda e: e.dma_start(out=dbg["adot"], in_=ADOT[:]), src=ADOT)
                            pg.dma(pg.sp, lambda e: e.dma_start(out=dbg["wgt"], in_=WGT[:]), src=WGT)
                            pg.dma(pg.sp, lambda e: e.dma_start(out=dbg["acc"], in_=ACC[:]), src=ACC)
                        V(lambda e, ti=ti: e.scalar_tensor_tensor(out=R[:], in0=H[ti][:], scalar=ALPHA, in1=ACC[:], op0=ALU.mult, op1=ALU.add), [H[ti], ACC], [R])
                        layer_norm(R[:], [R], H[ti][:], H[ti])
                        if last:
                            pg.dma(pg.sp, lambda e, ti=ti: e.dma_start(out=out[seq, ti * P:(ti + 1) * P, :], in_=H[ti][:]), src=H[ti])
                pg.barrier()

        XT = [pg.sbuf(f"XT{i}", [P, D], F32) for i in range(2)]
        for seq in range(nseq):
            load_ln_params(ln_in_g, ln_in_b)
            for ti in range(NT):
                xb = XT[ti % 2]
                LD(xb, xb[:], x[seq, ti * P:(ti + 1) * P, :])
                layer_norm(xb[:], [xb], H[ti][:], H[ti])
            for l in range(n_layers):
                last = (l == n_layers - 1)
                if do_attn:
                    attention_layer(l)
                if do_peer:
                    peer_layer(l, seq, last)
                elif last:
                    for ti in range(NT):
                        pg.dma(pg.sp, lambda e, ti=ti: e.dma_start(out=out[seq, ti * P:(ti + 1) * P, :], in_=H[ti][:]), src=H[ti])
        pg.finish()
        print("ninst", pg.ninst(), {e.name: e.ninst for e in pg.engs}, flush=True)
    return nc


def make_consts():
    ident = np.eye(P, dtype=np.float32)
    rows = S // 64
    row = np.repeat(np.arange(rows), 64)
    col = np.tile(np.arange(64), rows)
    pos = np.stack([row, col], -1).astype(np.float32)
    inv_freq = (np.float32(10000.0) ** (-np.arange(16, dtype=np.float32) / np.float32(16))).astype(np.float32)
    ang = (pos[:, :, None] * inv_freq).astype(np.float32)
    cos = np.cos(ang).astype(np.float32).reshape(NT, P, 32).transpose(1, 0, 2)
    sin = np.sin(ang).astype(np.float32).reshape(NT, P, 32).transpose(1, 0, 2)
    p = np.arange(P)[:, None]; c = np.arange(384)[None, :]
    dist = np.abs(p + 128 - c).astype(np.float32)
    negd = np.where(dist <= 128, -dist, np.float32(-1e30)).astype(np.float32)
    iota = np.tile(np.arange(16, dtype=np.float32)[None, :], (P, 1))
    return dict(c_ident=ident, c_cos=np.ascontiguousarray(cos), c_sin=np.ascontiguousarray(sin),
                c_negd=np.ascontiguousarray(negd), c_iota=np.ascontiguousarray(iota))


def prep_weights(inp):
    f = lambda a: np.ascontiguousarray(np.asarray(a, dtype=np.float32))
    gn = np.concatenate([np.asarray(inp["gn_a_g"]), np.asarray(inp["gn_b_g"])], -1)
    gnT = np.ascontiguousarray(gn.reshape(NL, 8, P).transpose(0, 2, 1)).astype(np.float32)
    d = dict(
        ln_in_g=f(inp["ln_in_g"]).reshape(1, D), ln_in_b=f(inp["ln_in_b"]).reshape(1, D),
        w_in=f(inp["w_in"]), qn_g=f(inp["qn_g"]), kn_g=f(inp["kn_g"]), sink=f(inp["sink"]), gnT=gnT,
        w_o=f(inp["w_o"]), ln1_g=f(inp["ln1_g"]), ln1_b=f(inp["ln1_b"]), peer_wq=f(inp["peer_wq"]),
        peer_keys=f(inp["peer_keys"]).reshape(NL, 16, P, P),
        ln2_g=f(inp["ln2_g"]), ln2_b=f(inp["ln2_b"]))
    pu = f(inp["peer_u"]); pv = f(inp["peer_v"])
    for i in range(NL):
        d[f"peer_u{i}"] = pu[i]
        d[f"peer_v{i}"] = pv[i]
    d.update(make_consts())
    return d


def kernel(**inputs):
    x = np.asarray(inputs["x"], dtype=np.float32)
    B = x.shape[0]
    nseq = B // NCORES
    wd = prep_weights(inputs)
    nc = build(nseq=nseq)
    in_maps = []
    for c in range(NCORES):
        m = dict(wd)
        m["x"] = np.ascontiguousarray(x[c * nseq:(c + 1) * nseq])
        in_maps.append(m)
    res = run_bass_kernel_spmd(nc, in_maps, core_ids=list(range(NCORES)))
    return np.concatenate([np.asarray(r["out"]) for r in res.results], axis=0).astype(np.float32)
```

```python
from contextlib import ExitStack

import concourse.bass as bass
import concourse.mybir as mybir

F32 = mybir.dt.float32
BF16 = mybir.dt.bfloat16
I32 = mybir.dt.int32
U32 = mybir.dt.uint32
ALU = mybir.AluOpType
AF = mybir.ActivationFunctionType
AX = mybir.AxisListType

EPOCH = 30000


class Eng:
    def __init__(self, prog, name, h, same_engine_sync=True):
        self.prog = prog
        self.name = name
        self.h = h
        self.sems = []
        self.count = 0
        self.waited = {}
        self.same = same_engine_sync
        self.ninst = 0
        self._new_epoch()

    def _new_epoch(self):
        s = self.prog.stack.enter_context(
            self.prog.nc.semaphore(f"s_{self.name}_{len(self.sems)}"))
        self.sems.append(s)
        self.sem = s
        self.count = 0

    def wait(self, sem, val):
        if self.waited.get(id(sem), 0) < val:
            self.h.wait_ge(sem, val)
            self.waited[id(sem)] = val
            self.ninst += 1


class Buf:
    def __init__(self, prog, t, name):
        self.prog = prog
        self.t = t
        self.name = name
        self.w = {}
        self.r = {}
        self.dslot = None
        self.stack = None

    def __getitem__(self, idx):
        return self.t[idx]

    def ap(self):
        return self.t[:]

    def _dslot(self):
        if self.dslot is None:
            pr = self.prog
            if pr.dfree:
                self.dslot = pr.dfree.pop()
            else:
                sem = pr.stack.enter_context(pr.nc.semaphore(f"d_{len(pr.dslots)}"))
                self.dslot = [sem, 0]
                pr.dslots.append(self.dslot)
            if self.stack is not None and self.stack is not pr.stack:
                slot = self.dslot
                self.stack.callback(lambda: pr.dfree.append(slot))
        return self.dslot


def _merge(deps, entries, skip_eng=None, skip_sem=None):
    for k, (sem, val, en) in entries.items():
        if skip_eng is not None and en == skip_eng:
            continue
        if skip_sem is not None and sem is skip_sem:
            continue
        if k not in deps or deps[k][1] < val:
            deps[k] = (sem, val, en)


class Prog:
    def __init__(self, nc, stack):
        self.nc = nc
        self.stack = stack
        self.dslots = []
        self.dfree = []
        self.pe = Eng(self, "pe", nc.tensor, same_engine_sync=False)
        self.dve = Eng(self, "dve", nc.vector)
        self.act = Eng(self, "act", nc.scalar)
        self.pool = Eng(self, "pool", nc.gpsimd)
        self.sp = Eng(self, "sp", nc.sync)
        self.engs = [self.pe, self.dve, self.act, self.pool, self.sp]
        self.nbuf = 0

    def sbuf(self, name, shape, dtype, stack=None):
        st = stack or self.stack
        t = st.enter_context(self.nc.sbuf_tensor(f"{name}_{self.nbuf}", list(shape), dtype))
        self.nbuf += 1
        b = Buf(self, t, name)
        b.stack = st
        return b

    def psum(self, name, shape, dtype=F32, stack=None):
        st = stack or self.stack
        t = st.enter_context(self.nc.psum_tensor(f"{name}_{self.nbuf}", list(shape), dtype))
        self.nbuf += 1
        return Buf(self, t, name)

    def op(self, eng, fn, reads=(), writes=()):
        deps = {}
        for b in reads:
            _merge(deps, b.w)
        for b in writes:
            _merge(deps, b.w)
            _merge(deps, b.r, skip_eng=eng.name)
        for k, (sem, val, en) in deps.items():
            if en == eng.name and not eng.same:
                continue
            eng.wait(sem, val)
        if eng.count >= EPOCH:
            eng._new_epoch()
        inst = fn(eng.h)
        eng.count += 1
        eng.ninst += 1
        inst.then_inc(eng.sem, 1)
        ent = (eng.sem, eng.count, eng.name)
        for b in reads:
            b.r[id(eng.sem)] = ent
        for b in writes:
            b.w = {id(eng.sem): ent}
            b.r = {}
        return inst

    def dma(self, q, fn, dst=None, src=None, extra_reads=()):
        own = dst if dst is not None else src
        slot = own._dslot()
        dsem = slot[0]
        deps = {}
        if dst is not None:
            _merge(deps, dst.w, skip_sem=dsem)
            _merge(deps, dst.r)
        if src is not None:
            _merge(deps, src.w)
        for b in extra_reads:
            _merge(deps, b.w)
        for k, (sem, val, en) in deps.items():
            q.wait(sem, val)
        inst = fn(q.h)
        q.ninst += 1
        slot[1] += 1
        inst.then_inc(dsem, 16)
        ent = (dsem, 16 * slot[1], "dma")
        if dst is not None:
            dst.w[id(dsem)] = ent
            dst.r = {}
        if src is not None:
            src.r[id(dsem)] = ent
        for b in extra_reads:
            b.r[id(dsem)] = ent
        return inst

    def barrier(self):
        for e in self.engs:
            for o in self.engs:
                if o is e:
                    continue
                if o.count == 0:
                    if len(o.sems) > 1:
                        e.wait(o.sems[-2], EPOCH)
                    continue
                e.wait(o.sem, o.count)
            for sl in self.dslots:
                if sl[1]:
                    e.wait(sl[0], 16 * sl[1])

    def finish(self):
        for sl in self.dslots:
            if sl[1]:
                self.sp.wait(sl[0], 16 * sl[1])
        for o in self.engs:
            if o is self.sp:
                continue
            if o.count == 0:
                if len(o.sems) > 1:
                    self.sp.wait(o.sems[-2], EPOCH)
                continue
            self.sp.wait(o.sem, o.count)

    def ninst(self):
        return sum(e.ninst for e in self.engs)

import numpy as np
from concourse.bass_utils import run_bass_kernel_spmd

P = 128
D = 1024
S = 2048
NT = 16
NL = 2
ALPHA = (2.0 * NL) ** 0.25
LN_EPS = 1e-5
RMS_EPS = 1e-6
NCORES = 8


def build(nseq=4, n_layers=2, do_attn=True, do_peer=True, peer_slots=128, debug=False):
    nc = bass.Bass("TRN2", target_bir_lowering=False)

    def dt(name, shape, dtype=F32, kind="ExternalInput"):
        return nc.dram_tensor(name, list(shape), dtype, kind=kind).ap()

    x = dt("x", [nseq, S, D])
    out = dt("out", [nseq, S, D], kind="ExternalOutput")
    ln_in_g = dt("ln_in_g", [1, D]); ln_in_b = dt("ln_in_b", [1, D])
    w_in = dt("w_in", [NL, D, 1536])
    qn_g = dt("qn_g", [NL, 64]); kn_g = dt("kn_g", [NL, 64])
    sink = dt("sink", [NL, 8])
    gnT = dt("gnT", [NL, P, 8])
    w_o = dt("w_o", [NL, D, D])
    ln1_g = dt("ln1_g", [NL, D]); ln1_b = dt("ln1_b", [NL, D])
    peer_wq = dt("peer_wq", [NL, D, 2048])
    peer_keys = dt("peer_keys", [NL, 16, P, P])
    peer_u = [dt(f"peer_u{i}", [16384, D]) for i in range(NL)]; peer_v = [dt(f"peer_v{i}", [16384, D]) for i in range(NL)]
    ln2_g = dt("ln2_g", [NL, D]); ln2_b = dt("ln2_b", [NL, D])
    c_ident = dt("c_ident", [P, P])
    c_cos = dt("c_cos", [P, NT, 32]); c_sin = dt("c_sin", [P, NT, 32])
    c_negd = dt("c_negd", [P, 384])
    c_iota = dt("c_iota", [P, 16])
    if debug:
        dbg = dict(sc=dt("dbg_sc", [P, 2048], F32, "ExternalOutput"), eidx=dt("dbg_eidx", [P, 128], I32, "ExternalOutput"),
                   gw=dt("dbg_gw", [P, 128], F32, "ExternalOutput"), adot=dt("dbg_adot", [P, 128], F32, "ExternalOutput"),
                   wgt=dt("dbg_wgt", [P, 128], F32, "ExternalOutput"), acc=dt("dbg_acc", [P, D], F32, "ExternalOutput"),
                   s16=dt("dbg_s16", [P, 256], F32, "ExternalOutput"), i16=dt("dbg_i16", [P, 256], F32, "ExternalOutput"),
                   tops=dt("dbg_tops", [P, 128], F32, "ExternalOutput"), pos=dt("dbg_pos", [P, 128], F32, "ExternalOutput"),
                   h=dt("dbg_h", [P, D], F32, "ExternalOutput"))

    with ExitStack() as st:
        pg = Prog(nc, st)
        V = lambda fn, r=(), w=(): pg.op(pg.dve, fn, r, w)
        A = lambda fn, r=(), w=(): pg.op(pg.act, fn, r, w)
        G = lambda fn, r=(), w=(): pg.op(pg.pool, fn, r, w)
        T = lambda fn, r=(), w=(): pg.op(pg.pe, fn, r, w)
        LD = lambda dst, dst_ap, src_ap: pg.dma(pg.sp, lambda e: e.dma_start(out=dst_ap, in_=src_ap), dst=dst)

        H = [pg.sbuf(f"H{i}", [P, D], F32) for i in range(NT)]
        ID = pg.sbuf("ID", [P, P], F32)
        IDb = pg.sbuf("IDb", [P, P], BF16)
        COS = pg.sbuf("COS", [P, NT, 32], F32)
        SIN = pg.sbuf("SIN", [P, NT, 32], F32)
        NEGD = pg.sbuf("NEGD", [P, 384], F32)
        IOTA = pg.sbuf("IOTA", [P, 16], F32)
        Gt = pg.sbuf("Gt", [P, D], F32)
        Bt = pg.sbuf("Bt", [P, D], F32)
        STt = pg.sbuf("STt", [P, 2, 6], F32)
        MV = pg.sbuf("MV", [P, 2], F32)
        RS = pg.sbuf("RS", [P, 1], F32)
        STG = [pg.sbuf(f"STG{i}", [P, 512], F32) for i in range(2)]
        HTt = pg.sbuf("HTt", [P, 8, P], BF16)
        R = pg.sbuf("R", [P, D], F32)
        PS0 = pg.psum("PS0", [P, 512]); PS1 = pg.psum("PS1", [P, 512])
        PS23 = pg.psum("PS23", [P, 1024]); PS45 = pg.psum("PS45", [P, 1024])
        PS6 = pg.psum("PS6", [P, 512]); PS7 = PS6
        PSB = pg.psum("PSB", [P, 1024], BF16)

        LD(ID, ID[:], c_ident)
        V(lambda e: e.tensor_copy(out=IDb[:], in_=ID[:]), [ID], [IDb])
        LD(COS, COS[:], c_cos); LD(SIN, SIN[:], c_sin); LD(NEGD, NEGD[:], c_negd); LD(IOTA, IOTA[:], c_iota)

        TBF = {}
        with ExitStack() as pro:
            CST = [pg.sbuf(f"CST{i}", [P, 4, D], F32, pro) for i in range(2)]
            CBF = [pg.sbuf(f"CBF{i}", [P, 4, D], BF16, pro) for i in range(2)]
            kk = 0
            for nm, tabs in (("u", peer_u), ("v", peer_v)):
                for i in range(NL):
                    sap = nc.dram_tensor(f"tbf_{nm}{i}", [16384, D], BF16, kind="Internal").ap()
                    TBF[(nm, i)] = sap
                    src_v = tabs[i].rearrange("(c p j) d -> c p j d", p=P, j=4)
                    dst_v = sap.rearrange("(c p j) d -> c p j d", p=P, j=4)
                    for c in range(16384 // (P * 4)):
                        a = CST[kk % 2]; b = CBF[kk % 2]
                        LD(a, a[:], src_v[c])
                        if kk % 2 == 0:
                            V(lambda e, a=a, b=b: e.tensor_copy(out=b[:], in_=a[:]), [a], [b])
                        else:
                            A(lambda e, a=a, b=b: e.copy(out=b[:], in_=a[:]), [a], [b])
                        pg.dma(pg.pool, lambda e, b=b, c=c, dst_v=dst_v: e.dma_start(out=dst_v[c], in_=b[:]), src=b)
                        kk += 1
        pg.barrier()

        stg_i = [0]

        def load_cast(dst, dst_ap, src_ap, ncols):
            sb = STG[stg_i[0] % 2]; k = stg_i[0]; stg_i[0] += 1
            LD(sb, sb[:, 0:ncols], src_ap)
            if k % 2 == 0:
                V(lambda e: e.tensor_copy(out=dst_ap, in_=sb[:, 0:ncols]), [sb], [dst])
            else:
                A(lambda e: e.copy(out=dst_ap, in_=sb[:, 0:ncols]), [sb], [dst])

        def layer_norm(src_ap, src_bufs, dst_ap, dst_buf):
            for c in range(2):
                V(lambda e, c=c: e.bn_stats(out=STt[:, c, :], in_=src_ap[:, c * 512:(c + 1) * 512]), src_bufs, [STt])
            V(lambda e: e.bn_aggr(out=MV[:], in_=STt[:]), [STt], [MV])
            V(lambda e: e.tensor_scalar(out=RS[:], in0=MV[:, 1:2], scalar1=LN_EPS, scalar2=None, op0=ALU.add), [MV], [RS])
            A(lambda e: e.sqrt(out=RS[:], in_=RS[:]), [RS], [RS])
            V(lambda e: e.reciprocal(out=RS[:], in_=RS[:]), [RS], [RS])
            V(lambda e: e.tensor_scalar(out=dst_ap, in0=src_ap, scalar1=MV[:, 0:1], scalar2=RS[:, 0:1],
                                        op0=ALU.subtract, op1=ALU.mult), list(src_bufs) + [MV, RS], [dst_buf])
            G(lambda e: e.tensor_tensor(out=dst_ap, in0=dst_ap, in1=Gt[:], op=ALU.mult), [dst_buf, Gt], [dst_buf])
            G(lambda e: e.tensor_tensor(out=dst_ap, in0=dst_ap, in1=Bt[:], op=ALU.add), [dst_buf, Bt], [dst_buf])

        def load_ln_params(g_ap, b_ap):
            LD(Gt, Gt[:], g_ap.partition_broadcast(P))
            LD(Bt, Bt[:], b_ap.partition_broadcast(P))

        def make_hT(hb):
            for c in range(8):
                ps = PS0 if c < 4 else PS1
                T(lambda e, c=c, ps=ps: e.transpose(out=ps[:, (c % 4) * P:(c % 4 + 1) * P], in_=hb[:, c * P:(c + 1) * P], identity=ID[:]),
                  [hb, ID], [ps])
            V(lambda e: e.tensor_copy(out=HTt[:, 0:4, :], in_=PS0[:].rearrange("p (c f) -> p c f", c=4)), [PS0], [HTt])
            A(lambda e: e.copy(out=HTt[:, 4:8, :], in_=PS1[:].rearrange("p (c f) -> p c f", c=4)), [PS1], [HTt])

        def rms_rope(ph, src, src_buf, nh, gain, ti, dst_view, dst_buf, sc):
            SQ, SSq, XN, TA, TB = sc["SQ"], sc["SSq"], sc["XN"], sc["TA"], sc["TB"]
            V(lambda e: e.tensor_tensor(out=SQ[:, 0:nh, :], in0=src, in1=src, op=ALU.mult), [src_buf], [SQ])
            V(lambda e: e.tensor_reduce(out=SSq[:, 0:nh], in_=SQ[:, 0:nh, :], axis=AX.X, op=ALU.add), [SQ], [SSq])
            V(lambda e: e.tensor_scalar(out=SSq[:, 0:nh], in0=SSq[:, 0:nh], scalar1=1.0 / 64, scalar2=RMS_EPS, op0=ALU.mult, op1=ALU.add), [SSq], [SSq])
            A(lambda e: e.sqrt(out=SSq[:, 0:nh], in_=SSq[:, 0:nh]), [SSq], [SSq])
            V(lambda e: e.reciprocal(out=SSq[:, 0:nh], in_=SSq[:, 0:nh]), [SSq], [SSq])
            V(lambda e: e.tensor_tensor(out=XN[:, 0:nh, :], in0=src, in1=SSq[:, 0:nh].unsqueeze(2).to_broadcast([P, nh, 64]), op=ALU.mult), [src_buf, SSq], [XN])
            G(lambda e: e.tensor_tensor(out=XN[:, 0:nh, :], in0=XN[:, 0:nh, :], in1=gain[:, :].unsqueeze(1).to_broadcast([P, nh, 64]), op=ALU.mult), [XN, gain], [XN])
            xv = XN[:, 0:nh, :].rearrange("p h (a b f) -> p h a b f", a=2, b=2)
            x1 = xv[:, :, :, 0, :]; x2 = xv[:, :, :, 1, :]
            cb = COS[:, ti, :].rearrange("p (a f) -> p a f", a=2).unsqueeze(1).to_broadcast([P, nh, 2, 16])
            sb_ = SIN[:, ti, :].rearrange("p (a f) -> p a f", a=2).unsqueeze(1).to_broadcast([P, nh, 2, 16])
            dv = dst_view.rearrange("p h (a b f) -> p h a b f", a=2, b=2)
            ta = TA[:, 0:nh, :].rearrange("p h (a f) -> p h a f", a=2)
            tb = TB[:, 0:nh, :].rearrange("p h (a f) -> p h a f", a=2)
            V(lambda e: e.tensor_tensor(out=ta, in0=x1, in1=cb, op=ALU.mult), [XN, COS], [TA])
            G(lambda e: e.tensor_tensor(out=tb, in0=x2, in1=sb_, op=ALU.mult), [XN, SIN], [TB])
            V(lambda e: e.tensor_tensor(out=dv[:, :, :, 0, :], in0=ta, in1=tb, op=ALU.subtract), [TA, TB], [dst_buf])
            V(lambda e: e.tensor_tensor(out=ta, in0=x1, in1=sb_, op=ALU.mult), [XN, SIN], [TA])
            G(lambda e: e.tensor_tensor(out=tb, in0=x2, in1=cb, op=ALU.mult), [XN, COS], [TB])
            V(lambda e: e.tensor_tensor(out=dv[:, :, :, 1, :], in0=ta, in1=tb, op=ALU.add), [TA, TB], [dst_buf])

        def attention_layer(l):
            with ExitStack() as ph:
                KT_A = pg.sbuf("KT_A", [P, 2, S], BF16, ph)
                KT_B = pg.sbuf("KT_B", [P, 2, S], BF16, ph)
                VA = pg.sbuf("VA", [P, NT, 2, 65], BF16, ph)
                VB = pg.sbuf("VB", [P, NT, 2, 64], BF16, ph)
                GQ = pg.sbuf("GQ", [P, 64], F32, ph)
                GK = pg.sbuf("GK", [P, 64], F32, ph)
                SINK = pg.sbuf("SINK", [P, 8], F32, ph)
                GOT = pg.sbuf("GOT", [P, 8], F32, ph)
                sc = dict(SQ=pg.sbuf("SQ", [P, 8, 64], F32, ph), SSq=pg.sbuf("SSq", [P, 8], F32, ph),
                          XN=pg.sbuf("XN", [P, 8, 64], F32, ph), TA=pg.sbuf("TA", [P, 8, 32], F32, ph),
                          TB=pg.sbuf("TB", [P, 8, 32], F32, ph))
                LD(GQ, GQ[:], qn_g[l:l + 1, :].partition_broadcast(P))
                V(lambda e: e.tensor_scalar(out=GQ[:], in0=GQ[:], scalar1=0.125, scalar2=None, op0=ALU.mult), [GQ], [GQ])
                LD(GK, GK[:], kn_g[l:l + 1, :].partition_broadcast(P))
                LD(SINK, SINK[:], sink[l:l + 1, :].partition_broadcast(P))
                LD(GOT, GOT[:], gnT[l])
                V(lambda e: e.memset(VA[:], 1.0), [], [VA])
                with ExitStack() as sp:
                    WKV = pg.sbuf("WKV", [P, 8, 512], BF16, sp)
                    KV32 = pg.sbuf("KV32", [P, 512], F32, sp)
                    KAb = pg.sbuf("KAb", [P, 2, 2, 64], BF16, sp)
                    KBb = pg.sbuf("KBb", [P, 2, 2, 64], BF16, sp)
                    for kc in range(8):
                        rows = slice(kc * P, (kc + 1) * P)
                        load_cast(WKV, WKV[:, kc, 0:256], w_in[l, rows, 512:768], 256)
                        load_cast(WKV, WKV[:, kc, 256:512], w_in[l, rows, 1280:1536], 256)
                    for ti in range(NT):
                        make_hT(H[ti])
                        for kc in range(8):
                            T(lambda e, kc=kc: e.matmul(PS6[:], lhsT=HTt[:, kc, :], rhs=WKV[:, kc, :], start=(kc == 0), stop=(kc == 7)),
                              [HTt, WKV], [PS6])
                        A(lambda e: e.copy(out=KV32[:], in_=PS6[:]), [PS6], [KV32])
                        ka = KV32[:, 0:128].rearrange("p (h d) -> p h d", h=2)
                        rms_rope(sp, ka, KV32, 2, GK, ti, KAb[:, :, 0, :], KAb, sc)
                        V(lambda e: e.tensor_copy(out=KAb[:, :, 1, :], in_=KAb[:, :, 0, :]), [KAb], [KAb])
                        kb = KV32[:, 256:384].rearrange("p (h d) -> p h d", h=2)
                        for dup in range(2):
                            V(lambda e, dup=dup: e.tensor_copy(out=KBb[:, :, dup, :], in_=kb), [KV32], [KBb])
                        A(lambda e, ti=ti: e.copy(out=VA[:, ti, :, 0:64], in_=KV32[:, 128:256].rearrange("p (h d) -> p h d", h=2)), [KV32], [VA])
                        A(lambda e, ti=ti: e.copy(out=VB[:, ti, :, :], in_=KV32[:, 384:512].rearrange("p (h d) -> p h d", h=2)), [KV32], [VB])
                        for kv in range(2):
                            T(lambda e, kv=kv: e.transpose(out=PSB[:, kv * P:(kv + 1) * P], in_=KAb[:, kv, :, :].rearrange("p a d -> p (a d)"), identity=IDb[:]),
                              [KAb, IDb], [PSB])
                            T(lambda e, kv=kv: e.transpose(out=PSB[:, (2 + kv) * P:(3 + kv) * P], in_=KBb[:, kv, :, :].rearrange("p a d -> p (a d)"), identity=IDb[:]),
                              [KBb, IDb], [PSB])
                        V(lambda e, ti=ti: e.tensor_copy(out=KT_A[:, :, ti * P:(ti + 1) * P], in_=PSB[:, 0:256].rearrange("p (k t) -> p k t", k=2)), [PSB], [KT_A])
                        V(lambda e, ti=ti: e.tensor_copy(out=KT_B[:, :, ti * P:(ti + 1) * P], in_=PSB[:, 256:512].rearrange("p (k t) -> p k t", k=2)), [PSB], [KT_B])
                pg.barrier()
                with ExitStack() as sg:
                    WQ = pg.sbuf("WQ", [P, 8, 1024], BF16, sg)
                    WO = pg.sbuf("WO", [P, 8, 1024], BF16, sg)
                    QA32 = pg.sbuf("QA32", [P, 512], F32, sg)
                    QAb = pg.sbuf("QAb", [P, 512], BF16, sg)
                    QBb = pg.sbuf("QBb", [P, 512], BF16, sg)
                    QTt = pg.sbuf("QTt", [P, 8, P], BF16, sg)
                    PTb = [pg.sbuf(f"PTb{i}", [P, 1024], BF16, sg) for i in range(2)]
                    OALL = pg.sbuf("OALL", [P, 8, 65], F32, sg)
                    RZ = pg.sbuf("RZ", [P, 8], F32, sg)
                    OT = pg.sbuf("OT", [P, 16, 64], F32, sg)
                    OSQ = pg.sbuf("OSQ", [P, 16, 64], F32, sg)
                    OSS = pg.sbuf("OSS", [P, 16], F32, sg)
                    OTb = pg.sbuf("OTb", [P, 1024], BF16, sg)
                    OTt = pg.sbuf("OTt", [P, 8, P], BF16, sg)
                    SB = pg.sbuf("SB", [P, 8, 384], F32, sg)
                    PB = pg.sbuf("PB", [P, 8, 384], BF16, sg)
                    PTB = [pg.sbuf(f"PTB{i}", [P, 384], BF16, sg) for i in range(2)]
                    MX = pg.sbuf("MX", [P, 8], F32, sg)
                    NM = pg.sbuf("NM", [P, 8], F32, sg)
                    ES = pg.sbuf("ES", [P, 8], F32, sg)
                    RSUM = pg.sbuf("RSUM", [P, 8], F32, sg)
                    ZB = pg.sbuf("ZB", [P, 8], F32, sg)
                    for kc in range(8):
                        rows = slice(kc * P, (kc + 1) * P)
                        load_cast(WQ, WQ[:, kc, 0:512], w_in[l, rows, 0:512], 512)
                        load_cast(WQ, WQ[:, kc, 512:1024], w_in[l, rows, 768:1280], 512)
                        load_cast(WO, WO[:, kc, 0:512], w_o[l, rows, 0:512], 512)
                        load_cast(WO, WO[:, kc, 512:1024], w_o[l, rows, 512:1024], 512)
                    load_ln_params(ln1_g[l:l + 1, :], ln1_b[l:l + 1, :])
                    for qt in range(NT):
                        make_hT(H[qt])
                        for nb in range(2):
                            for kc in range(8):
                                T(lambda e, nb=nb, kc=kc: e.matmul(PS23[:, nb * 512:(nb + 1) * 512], lhsT=HTt[:, kc, :], rhs=WQ[:, kc, nb * 512:(nb + 1) * 512],
                                                                   start=(kc == 0), stop=(kc == 7)), [HTt, WQ], [PS23])
                        A(lambda e: e.copy(out=QA32[:], in_=PS23[:, 0:512]), [PS23], [QA32])
                        A(lambda e: e.mul(out=QBb[:], in_=PS23[:, 512:1024], mul=0.125), [PS23], [QBb])
                        rms_rope(sg, QA32[:].rearrange("p (h d) -> p h d", h=8), QA32, 8, GQ, qt,
                                 QAb[:].rearrange("p (h d) -> p h d", h=8), QAb, sc)
                        for hp in range(4):
                            T(lambda e, hp=hp: e.transpose(out=PSB[:, hp * P:(hp + 1) * P], in_=QAb[:, hp * P:(hp + 1) * P], identity=IDb[:]), [QAb, IDb], [PSB])
                            T(lambda e, hp=hp: e.transpose(out=PSB[:, (4 + hp) * P:(5 + hp) * P], in_=QBb[:, hp * P:(hp + 1) * P], identity=IDb[:]), [QBb, IDb], [PSB])
                        V(lambda e: e.tensor_copy(out=QTt[:], in_=PSB[:].rearrange("p (c t) -> p c t", c=8)), [PSB], [QTt])
                        blocks = [(h, hs) for h in range(8) for hs in range(2)]
                        SPS = [PS23, PS45]
                        OPS = [PS0, PS1]

                        def emit_S(b):
                            h, hs = blocks[b]
                            kv, half, hp = h // 4, h % 2, h // 2
                            sps = SPS[b % 2]
                            pr = slice(half * 64, (half + 1) * 64)
                            for j in range(8):
                                stt = hs * 8 + j
                                T(lambda e, j=j, stt=stt: e.matmul(sps[:, j * P:(j + 1) * P], lhsT=KT_A[pr, kv, stt * P:(stt + 1) * P], rhs=QTt[pr, hp, :],
                                                                     start=True, stop=True), [KT_A, QTt], [sps])
                            A(lambda e: e.activation(out=PTb[b % 2][:], in_=sps[:], func=AF.Exp), [sps], [PTb[b % 2]])

                        def emit_PV(b):
                            h, hs = blocks[b]
                            kv = h // 4
                            ops = OPS[h // 4]
                            g = h % 4
                            for j in range(8):
                                stt = hs * 8 + j
                                T(lambda e, j=j, stt=stt: e.matmul(ops[:, g * 65:(g + 1) * 65], lhsT=PTb[b % 2][:, j * P:(j + 1) * P], rhs=VA[:, stt, kv, :],
                                                                     start=(stt == 0), stop=(stt == 15)), [PTb[b % 2], VA], [ops])
                            if hs == 1 and g == 3:
                                A(lambda e: e.copy(out=OALL[:, (h // 4) * 4:(h // 4) * 4 + 4, :], in_=ops[:, 0:260].rearrange("p (g c) -> p g c", g=4)), [ops], [OALL])

                        emit_S(0)
                        for b in range(len(blocks)):
                            if b + 1 < len(blocks):
                                emit_S(b + 1)
                            emit_PV(b)
                        V(lambda e: e.reciprocal(out=RZ[:], in_=OALL[:, :, 64]), [OALL], [RZ])
                        V(lambda e: e.tensor_tensor(out=OT[:, 0:8, :], in0=OALL[:, :, 0:64], in1=RZ[:, :].unsqueeze(2).to_broadcast([P, 8, 64]), op=ALU.mult), [OALL, RZ], [OT])
                        lo = max(qt - 1, 0); hi = min(qt + 1, NT - 1); nk = hi - lo + 1; ncol = nk * P
                        c0 = (lo - (qt - 1)) * P
                        for h in range(8):
                            kv, half, hp = h // 4, h % 2, h // 2
                            pr = slice(half * 64, (half + 1) * 64)
                            psb = PS6 if h % 2 == 0 else PS7
                            T(lambda e, psb=psb, kv=kv, hp=hp, pr=pr: e.matmul(psb[:, 0:ncol], lhsT=QTt[pr, 4 + hp, :], rhs=KT_B[pr, kv, lo * P:(hi + 1) * P], start=True, stop=True),
                              [QTt, KT_B], [psb])
                            slope = float(2.0 ** (-(h + 1)))
                            V(lambda e, psb=psb, h=h, slope=slope: e.scalar_tensor_tensor(out=SB[:, h, 0:ncol], in0=NEGD[:, c0:c0 + ncol], scalar=slope, in1=psb[:, 0:ncol],
                                                                                           op0=ALU.mult, op1=ALU.add), [NEGD, psb], [SB])
                        V(lambda e: e.tensor_reduce(out=MX[:], in_=SB[:, :, 0:ncol], axis=AX.X, op=ALU.max), [SB], [MX])
                        V(lambda e: e.tensor_tensor(out=MX[:], in0=MX[:], in1=SINK[:], op=ALU.max), [MX, SINK], [MX])
                        V(lambda e: e.tensor_scalar(out=NM[:], in0=MX[:], scalar1=-1.0, scalar2=None, op0=ALU.mult), [MX], [NM])
                        V(lambda e: e.tensor_tensor(out=ES[:], in0=SINK[:], in1=MX[:], op=ALU.subtract), [SINK, MX], [ES])
                        A(lambda e: e.activation(out=ES[:], in_=ES[:], func=AF.Exp), [ES], [ES])
                        for h in range(8):
                            A(lambda e, h=h: e.activation(out=PB[:, h, 0:ncol], in_=SB[:, h, 0:ncol], func=AF.Exp, bias=NM[:, h:h + 1], scale=1.0,
                                                          accum_out=RSUM[:, h:h + 1]), [SB, NM], [PB, RSUM])
                        V(lambda e: e.tensor_tensor(out=ZB[:], in0=RSUM[:], in1=ES[:], op=ALU.add), [RSUM, ES], [ZB])
                        V(lambda e: e.reciprocal(out=ZB[:], in_=ZB[:]), [ZB], [ZB])
                        for h in range(8):
                            kv = h // 4
                            ptb = PTB[h % 2]
                            for j in range(nk):
                                T(lambda e, h=h, j=j: e.transpose(out=PSB[:, j * P:(j + 1) * P], in_=PB[:, h, j * P:(j + 1) * P], identity=IDb[:]), [PB, IDb], [PSB])
                            V(lambda e, ptb=ptb: e.tensor_copy(out=ptb[:, 0:ncol], in_=PSB[:, 0:ncol]), [PSB], [ptb])
                            for j in range(nk):
                                T(lambda e, h=h, j=j, ptb=ptb, kv=kv: e.matmul(PS1[:, h * 64:(h + 1) * 64], lhsT=ptb[:, j * P:(j + 1) * P], rhs=VB[:, lo + j, kv, :],
                                                                                 start=(j == 0), stop=(j == nk - 1)), [ptb, VB], [PS1])
                        V(lambda e: e.tensor_tensor(out=OT[:, 8:16, :], in0=PS1[:].rearrange("p (h d) -> p h d", h=8), in1=ZB[:, :].unsqueeze(2).to_broadcast([P, 8, 64]), op=ALU.mult),
                          [PS1, ZB], [OT])
                        V(lambda e: e.tensor_tensor(out=OSQ[:], in0=OT[:], in1=OT[:], op=ALU.mult), [OT], [OSQ])
                        V(lambda e: e.tensor_reduce(out=OSS[:], in_=OSQ[:], axis=AX.X, op=ALU.add), [OSQ], [OSS])
                        V(lambda e: e.tensor_scalar(out=OSS[:], in0=OSS[:], scalar1=1.0 / 64, scalar2=RMS_EPS, op0=ALU.mult, op1=ALU.add), [OSS], [OSS])
                        A(lambda e: e.sqrt(out=OSS[:], in_=OSS[:]), [OSS], [OSS])
                        V(lambda e: e.reciprocal(out=OSS[:], in_=OSS[:]), [OSS], [OSS])
                        V(lambda e: e.tensor_tensor(out=OTb[:].rearrange("p (h d) -> p h d", h=16), in0=OT[:], in1=OSS[:, :].unsqueeze(2).to_broadcast([P, 16, 64]), op=ALU.mult),
                          [OT, OSS], [OTb])
                        for kc in range(8):
                            T(lambda e, kc=kc: e.transpose(out=PSB[:, kc * P:(kc + 1) * P], in_=OTb[:, kc * P:(kc + 1) * P], identity=IDb[:]), [OTb, IDb], [PSB])
                        V(lambda e: e.tensor_tensor(out=OTt[:], in0=PSB[:].rearrange("p (c t) -> p c t", c=8), in1=GOT[:, :].unsqueeze(2).to_broadcast([P, 8, P]), op=ALU.mult),
                          [PSB, GOT], [OTt])
                        for nb in range(2):
                            for kc in range(8):
                                T(lambda e, nb=nb, kc=kc: e.matmul(PS45[:, nb * 512:(nb + 1) * 512], lhsT=OTt[:, kc, :], rhs=WO[:, kc, nb * 512:(nb + 1) * 512],
                                                                   start=(kc == 0), stop=(kc == 7)), [OTt, WO], [PS45])
                        V(lambda e, qt=qt: e.scalar_tensor_tensor(out=R[:], in0=H[qt][:], scalar=ALPHA, in1=PS45[:], op0=ALU.mult, op1=ALU.add), [H[qt], PS45], [R])
                        layer_norm(R[:], [R], H[qt][:], H[qt])
                pg.barrier()

        def peer_layer(l, seq, last):
            NS = peer_slots
            with ExitStack() as ph:
                EIDX = pg.sbuf("EIDX", [P, NT, 128], I32, ph)
                GW = pg.sbuf("GW", [P, NT, 128], F32, ph)
                with ExitStack() as s1:
                    WQp = pg.sbuf("WQp", [P, 8, 2048], BF16, s1)
                    KEYST = pg.sbuf("KEYST", [P, 16, P], BF16, s1)
                    K32 = pg.sbuf("K32", [P, P], F32, s1)
                    QTp = pg.sbuf("QTp", [P, 16, P], BF16, s1)
                    SC = pg.sbuf("SC", [P, 16, P], F32, s1)
                    SC2 = pg.sbuf("SC2", [P, 16, P], F32, s1)
                    S16 = pg.sbuf("S16", [P, 16, 16], F32, s1)
                    I16u = pg.sbuf("I16u", [P, 16, 16], U32, s1)
                    I16f = pg.sbuf("I16f", [P, 16, 16], F32, s1)
                    CAND = pg.sbuf("CAND", [P, 8, 256], F32, s1)
                    CAND2 = pg.sbuf("CAND2", [P, 8, 256], F32, s1)
                    TOPS = pg.sbuf("TOPS", [P, 8, 16], F32, s1)
                    POSu = pg.sbuf("POSu", [P, 8, 16], U32, s1)
                    POSf = pg.sbuf("POSf", [P, 8, 16], F32, s1)
                    AFl = pg.sbuf("AFl", [P, 8, 16], F32, s1)
                    BFl = pg.sbuf("BFl", [P, 8, 16], F32, s1)
                    OH = pg.sbuf("OH", [P, 8, 16, 16], F32, s1)
                    I1S = pg.sbuf("I1S", [P, 8, 16], F32, s1)
                    I2S = pg.sbuf("I2S", [P, 8, 16], F32, s1)
                    EF = pg.sbuf("EF", [P, 8, 16], F32, s1)
                    MXT = pg.sbuf("MXT", [P, 8], F32, s1)
                    TS = pg.sbuf("TS", [P, 8, 16], F32, s1)
                    SM = pg.sbuf("SM", [P, 8], F32, s1)
                    for kc in range(8):
                        rows = slice(kc * P, (kc + 1) * P)
                        for cc in range(4):
                            load_cast(WQp, WQp[:, kc, cc * 512:(cc + 1) * 512], peer_wq[l, rows, cc * 512:(cc + 1) * 512], 512)
                    for hp in range(16):
                        LD(K32, K32[:], peer_keys[l, hp])
                        T(lambda e: e.transpose(out=PS6[:, 0:P], in_=K32[:], identity=ID[:]), [K32, ID], [PS6])
                        V(lambda e, hp=hp: e.tensor_copy(out=KEYST[:, hp, :], in_=PS6[:, 0:P]), [PS6], [KEYST])
                    iota4 = IOTA[:, :].unsqueeze(1).unsqueeze(1).to_broadcast([P, 8, 16, 16])
                    for ti in range(NT):
                        make_hT(H[ti])
                        for hp in range(16):
                            ps = PS23 if (hp // 4) % 2 == 0 else PS45
                            for kc in range(8):
                                T(lambda e, hp=hp, kc=kc, ps=ps: e.matmul(ps[:, (hp % 4) * P:(hp % 4 + 1) * P], lhsT=WQp[:, kc, hp * P:(hp + 1) * P], rhs=HTt[:, kc, :],
                                                                          start=(kc == 0), stop=(kc == 7)), [WQp, HTt], [ps])
                            if hp % 4 == 3:
                                g4 = hp // 4
                                if g4 % 2 == 0:
                                    V(lambda e, g4=g4, ps=ps: e.tensor_copy(out=QTp[:, g4 * 4:g4 * 4 + 4, :], in_=ps[:, 0:512].rearrange("p (c t) -> p c t", c=4)), [ps], [QTp])
                                else:
                                    A(lambda e, g4=g4, ps=ps: e.copy(out=QTp[:, g4 * 4:g4 * 4 + 4, :], in_=ps[:, 0:512].rearrange("p (c t) -> p c t", c=4)), [ps], [QTp])
                        for hp in range(16):
                            ps = PS6 if (hp // 4) % 2 == 0 else PS0
                            T(lambda e, hp=hp, ps=ps: e.matmul(ps[:, (hp % 4) * P:(hp % 4 + 1) * P], lhsT=QTp[:, hp, :], rhs=KEYST[:, hp, :], start=True, stop=True),
                              [QTp, KEYST], [ps])
                            if hp % 4 == 3:
                                g4 = hp // 4
                                A(lambda e, g4=g4, ps=ps: e.copy(out=SC[:, g4 * 4:g4 * 4 + 4, :], in_=ps[:].rearrange("p (c t) -> p c t", c=4)), [ps], [SC])

                        def top16(vals, vals2, vbufs, outv, outi, obufs):
                            V(lambda e: e.max(out=outv[:, 0:8], in_=vals), vbufs[:1], [obufs[0]])
                            V(lambda e: e.max_index(out=outi[:, 0:8], in_max=outv[:, 0:8], in_values=vals), [vbufs[0], obufs[0]], [obufs[1]])
                            V(lambda e: e.match_replace(out=vals2, in_to_replace=outv[:, 0:8], in_values=vals, imm_value=-1e30), [vbufs[0], obufs[0]], [vbufs[1]])
                            V(lambda e: e.max(out=outv[:, 8:16], in_=vals2), [vbufs[1]], [obufs[0]])
                            V(lambda e: e.max_index(out=outi[:, 8:16], in_max=outv[:, 8:16], in_values=vals2), [vbufs[1], obufs[0]], [obufs[1]])

                        for hp in range(16):
                            top16(SC[:, hp, :], SC2[:, hp, :], [SC, SC2], S16[:, hp, :], I16u[:, hp, :], [S16, I16u])
                        s16v = S16[:].rearrange("p (h q) k -> p h q k", q=2)
                        V(lambda e: e.tensor_tensor(out=CAND[:].rearrange("p h (a b) -> p h a b", a=16),
                                                    in0=s16v[:, :, 0, :].unsqueeze(3).to_broadcast([P, 8, 16, 16]),
                                                    in1=s16v[:, :, 1, :].unsqueeze(2).to_broadcast([P, 8, 16, 16]), op=ALU.add), [S16], [CAND])
                        for h in range(8):
                            top16(CAND[:, h, :], CAND2[:, h, :], [CAND, CAND2], TOPS[:, h, :], POSu[:, h, :], [TOPS, POSu])
                        V(lambda e: e.tensor_copy(out=I16f[:], in_=I16u[:]), [I16u], [I16f])
                        V(lambda e: e.tensor_copy(out=POSf[:], in_=POSu[:]), [POSu], [POSf])
                        V(lambda e: e.tensor_scalar(out=AFl[:], in0=POSf[:], scalar1=0.0625, scalar2=-0.46875, op0=ALU.mult, op1=ALU.add), [POSf], [AFl])
                        V(lambda e: e.tensor_scalar(out=AFl[:], in0=AFl[:], scalar1=12582912.0, scalar2=-12582912.0, op0=ALU.add, op1=ALU.add), [AFl], [AFl])
                        V(lambda e: e.scalar_tensor_tensor(out=BFl[:], in0=AFl[:], scalar=-16.0, in1=POSf[:], op0=ALU.mult, op1=ALU.add), [AFl, POSf], [BFl])
                        i16v = I16f[:].rearrange("p (h q) k -> p h q k", q=2)
                        for (sel, qq, dst) in ((AFl, 0, I1S), (BFl, 1, I2S)):
                            V(lambda e, sel=sel: e.tensor_tensor(out=OH[:], in0=iota4, in1=sel[:, :, :].unsqueeze(3).to_broadcast([P, 8, 16, 16]), op=ALU.is_equal), [IOTA, sel], [OH])
                            V(lambda e, qq=qq: e.tensor_tensor(out=OH[:], in0=OH[:], in1=i16v[:, :, qq, :].unsqueeze(2).to_broadcast([P, 8, 16, 16]), op=ALU.mult), [OH, I16f], [OH])
                            V(lambda e, dst=dst: e.tensor_reduce(out=dst[:], in_=OH[:], axis=AX.X, op=ALU.add), [OH], [dst])
                        V(lambda e: e.scalar_tensor_tensor(out=EF[:], in0=I1S[:], scalar=128.0, in1=I2S[:], op0=ALU.mult, op1=ALU.add), [I1S, I2S], [EF])
                        V(lambda e, ti=ti: e.tensor_copy(out=EIDX[:, ti, :].rearrange("p (h k) -> p h k", h=8), in_=EF[:]), [EF], [EIDX])
                        V(lambda e: e.tensor_reduce(out=MXT[:], in_=TOPS[:], axis=AX.X, op=ALU.max), [TOPS], [MXT])
                        V(lambda e: e.tensor_tensor(out=TS[:], in0=TOPS[:], in1=MXT[:, :].unsqueeze(2).to_broadcast([P, 8, 16]), op=ALU.subtract), [TOPS, MXT], [TS])
                        A(lambda e: e.activation(out=TS[:], in_=TS[:], func=AF.Exp), [TS], [TS])
                        V(lambda e: e.tensor_reduce(out=SM[:], in_=TS[:], axis=AX.X, op=ALU.add), [TS], [SM])
                        V(lambda e: e.reciprocal(out=SM[:], in_=SM[:]), [SM], [SM])
                        V(lambda e, ti=ti: e.tensor_tensor(out=GW[:, ti, :].rearrange("p (h k) -> p h k", h=8), in0=TS[:], in1=SM[:, :].unsqueeze(2).to_broadcast([P, 8, 16]), op=ALU.mult),
                          [TS, SM], [GW])
                        if debug and ti == 0 and l == 0:
                            pg.dma(pg.sp, lambda e: e.dma_start(out=dbg["sc"], in_=SC[:].rearrange("p a b -> p (a b)")), src=SC)
                            pg.dma(pg.sp, lambda e: e.dma_start(out=dbg["s16"], in_=S16[:].rearrange("p a b -> p (a b)")), src=S16)
                            pg.dma(pg.sp, lambda e: e.dma_start(out=dbg["i16"], in_=I16f[:].rearrange("p a b -> p (a b)")), src=I16f)
                            pg.dma(pg.sp, lambda e: e.dma_start(out=dbg["tops"], in_=TOPS[:].rearrange("p a b -> p (a b)")), src=TOPS)
                            pg.dma(pg.sp, lambda e: e.dma_start(out=dbg["pos"], in_=POSf[:].rearrange("p a b -> p (a b)")), src=POSf)
                            pg.dma(pg.sp, lambda e: e.dma_start(out=dbg["eidx"], in_=EIDX[:, 0, :]), src=EIDX)
                            pg.dma(pg.sp, lambda e: e.dma_start(out=dbg["gw"], in_=GW[:, 0, :]), src=GW)
                            pg.dma(pg.sp, lambda e: e.dma_start(out=dbg["h"], in_=H[0][:]), src=H[0])
                pg.barrier()
                with ExitStack() as s2:
                    GS = 4
                    NB = 3
                    UB = [pg.sbuf(f"UB{i}", [P, GS, D], BF16, s2) for i in range(NB)]
                    VBf = [pg.sbuf(f"VBf{i}", [P, GS, D], BF16, s2) for i in range(NB)]
                    JUNK = pg.sbuf("JUNK", [P, D], F32, s2)
                    ADOT = pg.sbuf("ADOT", [P, 128], F32, s2)
                    WGT = pg.sbuf("WGT", [P, 128], F32, s2)
                    ACC = pg.sbuf("ACC", [P, D], F32, s2)
                    load_ln_params(ln2_g[l:l + 1, :], ln2_b[l:l + 1, :])
                    ng = NS // GS
                    for ti in range(NT):
                        if NS < 128:
                            V(lambda e: e.memset(ADOT[:], 0.0), [], [ADOT])
                        for tab, bufs, which in ((TBF[("u", l)], UB, 0), (TBF[("v", l)], VBf, 1)):
                            if which == 1:
                                A(lambda e: e.activation(out=WGT[:], in_=ADOT[:], func=AF.Gelu), [ADOT], [WGT])
                                V(lambda e, ti=ti: e.tensor_tensor(out=WGT[:], in0=WGT[:], in1=GW[:, ti, :], op=ALU.mult), [WGT, GW], [WGT])
                            for grp in range(ng):
                                bb = bufs[grp % NB]
                                for jj in range(GS):
                                    j = grp * GS + jj
                                    pg.dma(pg.pool, lambda e, bb=bb, jj=jj, j=j, ti=ti, tab=tab: e.indirect_dma_start(
                                        out=bb[:, jj, :], out_offset=None, in_=tab,
                                        in_offset=bass.IndirectOffsetOnAxis(ap=EIDX[:, ti, j:j + 1], axis=0)), dst=bb, extra_reads=[EIDX])
                                for jj in range(GS):
                                    j = grp * GS + jj
                                    if which == 0:
                                        V(lambda e, bb=bb, jj=jj, j=j, ti=ti: e.scalar_tensor_tensor(out=JUNK[:], in0=bb[:, jj, :], scalar=1.0, in1=H[ti][:], op0=ALU.mult, op1=ALU.mult,
                                                                                                     accum_out=ADOT[:, j:j + 1]), [bb, H[ti]], [JUNK, ADOT])
                                    elif j == 0:
                                        V(lambda e, bb=bb, jj=jj, j=j: e.tensor_scalar(out=ACC[:], in0=bb[:, jj, :], scalar1=WGT[:, j:j + 1], scalar2=None, op0=ALU.mult), [bb, WGT], [ACC])
                                    else:
                                        V(lambda e, bb=bb, jj=jj, j=j: e.scalar_tensor_tensor(out=ACC[:], in0=bb[:, jj, :], scalar=WGT[:, j:j + 1], in1=ACC[:], op0=ALU.mult, op1=ALU.add),
                                          [bb, WGT, ACC], [ACC])
                        if debug and ti == 0 and l == 0:
                            pg.dma(pg.sp, lambda e: e.dma_start(out=dbg["adot"], in_=ADOT[:]), src=ADOT)
                            pg.dma(pg.sp, lambda e: e.dma_start(out=dbg["wgt"], in_=WGT[:]), src=WGT)
                            pg.dma(pg.sp, lambda e: e.dma_start(out=dbg["acc"], in_=ACC[:]), src=ACC)
                        V(lambda e, ti=ti: e.scalar_tensor_tensor(out=R[:], in0=H[ti][:], scalar=ALPHA, in1=ACC[:], op0=ALU.mult, op1=ALU.add), [H[ti], ACC], [R])
                        layer_norm(R[:], [R], H[ti][:], H[ti])
                        if last:
                            pg.dma(pg.sp, lambda e, ti=ti: e.dma_start(out=out[seq, ti * P:(ti + 1) * P, :], in_=H[ti][:]), src=H[ti])
                pg.barrier()

        XT = [pg.sbuf(f"XT{i}", [P, D], F32) for i in range(2)]
        for seq in range(nseq):
            load_ln_params(ln_in_g, ln_in_b)
            for ti in range(NT):
                xb = XT[ti % 2]
                LD(xb, xb[:], x[seq, ti * P:(ti + 1) * P, :])
                layer_norm(xb[:], [xb], H[ti][:], H[ti])
            for l in range(n_layers):
                last = (l == n_layers - 1)
                if do_attn:
                    attention_layer(l)
                if do_peer:
                    peer_layer(l, seq, last)
                elif last:
                    for ti in range(NT):
                        pg.dma(pg.sp, lambda e, ti=ti: e.dma_start(out=out[seq, ti * P:(ti + 1) * P, :], in_=H[ti][:]), src=H[ti])
        pg.finish()
        print("ninst", pg.ninst(), {e.name: e.ninst for e in pg.engs}, flush=True)
    return nc


def make_consts():
    ident = np.eye(P, dtype=np.float32)
    rows = S // 64
    row = np.repeat(np.arange(rows), 64)
    col = np.tile(np.arange(64), rows)
    pos = np.stack([row, col], -1).astype(np.float32)
    inv_freq = (np.float32(10000.0) ** (-np.arange(16, dtype=np.float32) / np.float32(16))).astype(np.float32)
    ang = (pos[:, :, None] * inv_freq).astype(np.float32)
    cos = np.cos(ang).astype(np.float32).reshape(NT, P, 32).transpose(1, 0, 2)
    sin = np.sin(ang).astype(np.float32).reshape(NT, P, 32).transpose(1, 0, 2)
    p = np.arange(P)[:, None]; c = np.arange(384)[None, :]
    dist = np.abs(p + 128 - c).astype(np.float32)
    negd = np.where(dist <= 128, -dist, np.float32(-1e30)).astype(np.float32)
    iota = np.tile(np.arange(16, dtype=np.float32)[None, :], (P, 1))
    return dict(c_ident=ident, c_cos=np.ascontiguousarray(cos), c_sin=np.ascontiguousarray(sin),
                c_negd=np.ascontiguousarray(negd), c_iota=np.ascontiguousarray(iota))


def prep_weights(inp):
    f = lambda a: np.ascontiguousarray(np.asarray(a, dtype=np.float32))
    gn = np.concatenate([np.asarray(inp["gn_a_g"]), np.asarray(inp["gn_b_g"])], -1)
    gnT = np.ascontiguousarray(gn.reshape(NL, 8, P).transpose(0, 2, 1)).astype(np.float32)
    d = dict(
        ln_in_g=f(inp["ln_in_g"]).reshape(1, D), ln_in_b=f(inp["ln_in_b"]).reshape(1, D),
        w_in=f(inp["w_in"]), qn_g=f(inp["qn_g"]), kn_g=f(inp["kn_g"]), sink=f(inp["sink"]), gnT=gnT,
        w_o=f(inp["w_o"]), ln1_g=f(inp["ln1_g"]), ln1_b=f(inp["ln1_b"]), peer_wq=f(inp["peer_wq"]),
        peer_keys=f(inp["peer_keys"]).reshape(NL, 16, P, P),
        ln2_g=f(inp["ln2_g"]), ln2_b=f(inp["ln2_b"]))
    pu = f(inp["peer_u"]); pv = f(inp["peer_v"])
    for i in range(NL):
        d[f"peer_u{i}"] = pu[i]
        d[f"peer_v{i}"] = pv[i]
    d.update(make_consts())
    return d


def kernel(**inputs):
    x = np.asarray(inputs["x"], dtype=np.float32)
    B = x.shape[0]
    nseq = B // NCORES
    wd = prep_weights(inputs)
    nc = build(nseq=nseq)
    in_maps = []
    for c in range(NCORES):
        m = dict(wd)
        m["x"] = np.ascontiguousarray(x[c * nseq:(c + 1) * nseq])
        in_maps.append(m)
    res = run_bass_kernel_spmd(nc, in_maps, core_ids=list(range(NCORES)))
    return np.concatenate([np.asarray(r["out"]) for r in res.results], axis=0).astype(np.float32)
```

```python
from contextlib import ExitStack

import concourse.bass as bass
import concourse.mybir as mybir

F32 = mybir.dt.float32
BF16 = mybir.dt.bfloat16
I32 = mybir.dt.int32
U32 = mybir.dt.uint32
ALU = mybir.AluOpType
AF = mybir.ActivationFunctionType
AX = mybir.AxisListType

EPOCH = 30000


class Eng:
    def __init__(self, prog, name, h, same_engine_sync=True):
        self.prog = prog
        self.name = name
        self.h = h
        self.sems = []
        self.count = 0
        self.waited = {}
        self.same = same_engine_sync
        self.ninst = 0
        self._new_epoch()

    def _new_epoch(self):
        s = self.prog.stack.enter_context(
            self.prog.nc.semaphore(f"s_{self.name}_{len(self.sems)}"))
        self.sems.append(s)
        self.sem = s
        self.count = 0

    def wait(self, sem, val):
        if self.waited.get(id(sem), 0) < val:
            self.h.wait_ge(sem, val)
            self.waited[id(sem)] = val
            self.ninst += 1


class Buf:
    def __init__(self, prog, t, name):
        self.prog = prog
        self.t = t
        self.name = name
        self.w = {}
        self.r = {}
        self.dslot = None
        self.stack = None

    def __getitem__(self, idx):
        return self.t[idx]

    def ap(self):
        return self.t[:]

    def _dslot(self, kind):
        if self.dslot is not None:
            assert self.dkind == kind, (self.name, self.dkind, kind)
        if self.dslot is None:
            pr = self.prog
            self.dkind = kind
            if pr.dfree[kind]:
                self.dslot = pr.dfree[kind].pop()
            else:
                sem = pr.stack.enter_context(pr.nc.semaphore(f"d_{len(pr.dslots)}"))
                self.dslot = [sem, 0]
                pr.dslots.append(self.dslot)
            if self.stack is not None and self.stack is not pr.stack:
                slot = self.dslot
                self.stack.callback(lambda: pr.dfree[kind].append(slot))
        return self.dslot


def _merge(deps, entries, skip_eng=None, skip_sem=None):
    for k, (sem, val, en) in entries.items():
        if skip_eng is not None and en == skip_eng:
            continue
        if skip_sem is not None and sem is skip_sem:
            continue
        if k not in deps or deps[k][1] < val:
            deps[k] = (sem, val, en)


class Prog:
    def __init__(self, nc, stack):
        self.nc = nc
        self.stack = stack
        self.dslots = []
        self.dfree = {"hw": [], "sw": []}
        self.pe = Eng(self, "pe", nc.tensor, same_engine_sync=False)
        self.dve = Eng(self, "dve", nc.vector)
        self.act = Eng(self, "act", nc.scalar)
        self.pool = Eng(self, "pool", nc.gpsimd)
        self.sp = Eng(self, "sp", nc.sync)
        self.engs = [self.pe, self.dve, self.act, self.pool, self.sp]
        self.nbuf = 0

    def sbuf(self, name, shape, dtype, stack=None):
        st = stack or self.stack
        t = st.enter_context(self.nc.sbuf_tensor(f"{name}_{self.nbuf}", list(shape), dtype))
        self.nbuf += 1
        b = Buf(self, t, name)
        b.stack = st
        return b

    def psum(self, name, shape, dtype=F32, stack=None):
        st = stack or self.stack
        t = st.enter_context(self.nc.psum_tensor(f"{name}_{self.nbuf}", list(shape), dtype))
        self.nbuf += 1
        return Buf(self, t, name)

    def op(self, eng, fn, reads=(), writes=()):
        deps = {}
        for b in reads:
            _merge(deps, b.w)
        for b in writes:
            _merge(deps, b.w)
            _merge(deps, b.r, skip_eng=eng.name)
        for k, (sem, val, en) in deps.items():
            if en == eng.name and not eng.same:
                continue
            eng.wait(sem, val)
        if eng.count >= EPOCH:
            eng._new_epoch()
        inst = fn(eng.h)
        eng.count += 1
        eng.ninst += 1
        inst.then_inc(eng.sem, 1)
        ent = (eng.sem, eng.count, eng.name)
        for b in reads:
            b.r[id(eng.sem)] = ent
        for b in writes:
            b.w = {id(eng.sem): ent}
            b.r = {}
        return inst

    def dma(self, q, fn, dst=None, src=None, extra_reads=()):
        own = dst if dst is not None else src
        slot = own._dslot("sw" if q is self.pool else "hw")
        dsem = slot[0]
        deps = {}
        if dst is not None:
            _merge(deps, dst.w, skip_sem=dsem)
            _merge(deps, dst.r)
        if src is not None:
            _merge(deps, src.w)
        for b in extra_reads:
            _merge(deps, b.w)
        for k, (sem, val, en) in deps.items():
            q.wait(sem, val)
        inst = fn(q.h)
        q.ninst += 1
        slot[1] += 1
        inst.then_inc(dsem, 16)
        ent = (dsem, 16 * slot[1], "dma")
        if dst is not None:
            dst.w[id(dsem)] = ent
            dst.r = {}
        if src is not None:
            src.r[id(dsem)] = ent
        for b in extra_reads:
            b.r[id(dsem)] = ent
        return inst

    def barrier(self):
        for e in self.engs:
            for o in self.engs:
                if o is e:
                    continue
                if o.count == 0:
                    if len(o.sems) > 1:
                        e.wait(o.sems[-2], EPOCH)
                    continue
                e.wait(o.sem, o.count)
            for sl in self.dslots:
                if sl[1]:
                    e.wait(sl[0], 16 * sl[1])

    def finish(self):
        for sl in self.dslots:
            if sl[1]:
                self.sp.wait(sl[0], 16 * sl[1])
        for o in self.engs:
            if o is self.sp:
                continue
            if o.count == 0:
                if len(o.sems) > 1:
                    self.sp.wait(o.sems[-2], EPOCH)
                continue
            self.sp.wait(o.sem, o.count)

    def ninst(self):
        return sum(e.ninst for e in self.engs)

import numpy as np
from concourse.bass_utils import run_bass_kernel_spmd

P = 128
D = 1024
S = 2048
NT = 16
NL = 2
ALPHA = (2.0 * NL) ** 0.25
LN_EPS = 1e-5
RMS_EPS = 1e-6
NCORES = 8


def build(nseq=4, n_layers=2, do_attn=True, do_peer=True, peer_slots=128, debug=False):
    nc = bass.Bass("TRN2", target_bir_lowering=False)

    def dt(name, shape, dtype=F32, kind="ExternalInput"):
        return nc.dram_tensor(name, list(shape), dtype, kind=kind).ap()

    x = dt("x", [nseq, S, D])
    out = dt("out", [nseq, S, D], kind="ExternalOutput")
    ln_in_g = dt("ln_in_g", [1, D]); ln_in_b = dt("ln_in_b", [1, D])
    w_in = dt("w_in", [NL, D, 1536])
    qn_g = dt("qn_g", [NL, 64]); kn_g = dt("kn_g", [NL, 64])
    sink = dt("sink", [NL, 8])
    gnT = dt("gnT", [NL, P, 8])
    w_o = dt("w_o", [NL, D, D])
    ln1_g = dt("ln1_g", [NL, D]); ln1_b = dt("ln1_b", [NL, D])
    peer_wq = dt("peer_wq", [NL, D, 2048])
    peer_keys = dt("peer_keys", [NL, 16, P, P])
    peer_u = [dt(f"peer_u{i}", [16384, D]) for i in range(NL)]; peer_v = [dt(f"peer_v{i}", [16384, D]) for i in range(NL)]
    ln2_g = dt("ln2_g", [NL, D]); ln2_b = dt("ln2_b", [NL, D])
    c_ident = dt("c_ident", [P, P])
    c_cos = dt("c_cos", [P, NT, 32]); c_sin = dt("c_sin", [P, NT, 32])
    c_negd = dt("c_negd", [P, 384])
    c_iota = dt("c_iota", [P, 16])
    if debug:
        dbg = dict(sc=dt("dbg_sc", [P, 2048], F32, "ExternalOutput"), eidx=dt("dbg_eidx", [P, 128], I32, "ExternalOutput"),
                   gw=dt("dbg_gw", [P, 128], F32, "ExternalOutput"), adot=dt("dbg_adot", [P, 128], F32, "ExternalOutput"),
                   wgt=dt("dbg_wgt", [P, 128], F32, "ExternalOutput"), acc=dt("dbg_acc", [P, D], F32, "ExternalOutput"),
                   s16=dt("dbg_s16", [P, 256], F32, "ExternalOutput"), i16=dt("dbg_i16", [P, 256], F32, "ExternalOutput"),
                   tops=dt("dbg_tops", [P, 128], F32, "ExternalOutput"), pos=dt("dbg_pos", [P, 128], F32, "ExternalOutput"),
                   h=dt("dbg_h", [P, D], F32, "ExternalOutput"))

    with ExitStack() as st:
        pg = Prog(nc, st)
        V = lambda fn, r=(), w=(): pg.op(pg.dve, fn, r, w)
        A = lambda fn, r=(), w=(): pg.op(pg.act, fn, r, w)
        G = lambda fn, r=(), w=(): pg.op(pg.pool, fn, r, w)
        T = lambda fn, r=(), w=(): pg.op(pg.pe, fn, r, w)
        LD = lambda dst, dst_ap, src_ap: pg.dma(pg.sp, lambda e: e.dma_start(out=dst_ap, in_=src_ap), dst=dst)

        H = [pg.sbuf(f"H{i}", [P, D], F32) for i in range(NT)]
        ID = pg.sbuf("ID", [P, P], F32)
        IDb = pg.sbuf("IDb", [P, P], BF16)
        COS = pg.sbuf("COS", [P, NT, 32], F32)
        SIN = pg.sbuf("SIN", [P, NT, 32], F32)
        NEGD = pg.sbuf("NEGD", [P, 384], F32)
        IOTA = pg.sbuf("IOTA", [P, 16], F32)
        Gt = pg.sbuf("Gt", [P, D], F32)
        Bt = pg.sbuf("Bt", [P, D], F32)
        STt = pg.sbuf("STt", [P, 2, 6], F32)
        MV = pg.sbuf("MV", [P, 2], F32)
        RS = pg.sbuf("RS", [P, 1], F32)
        STG = [pg.sbuf(f"STG{i}", [P, 512], F32) for i in range(2)]
        HTt = pg.sbuf("HTt", [P, 8, P], BF16)
        R = pg.sbuf("R", [P, D], F32)
        PS0 = pg.psum("PS0", [P, 512]); PS1 = pg.psum("PS1", [P, 512])
        PS23 = pg.psum("PS23", [P, 1024]); PS45 = pg.psum("PS45", [P, 1024])
        PS6 = pg.psum("PS6", [P, 512]); PS7 = PS6
        PSB = pg.psum("PSB", [P, 1024], BF16)

        LD(ID, ID[:], c_ident)
        V(lambda e: e.tensor_copy(out=IDb[:], in_=ID[:]), [ID], [IDb])
        LD(COS, COS[:], c_cos); LD(SIN, SIN[:], c_sin); LD(NEGD, NEGD[:], c_negd); LD(IOTA, IOTA[:], c_iota)

        TBF = {}
        with ExitStack() as pro:
            CST = [pg.sbuf(f"CST{i}", [P, 4, D], F32, pro) for i in range(2)]
            CBF = [pg.sbuf(f"CBF{i}", [P, 4, D], BF16, pro) for i in range(2)]
            kk = 0
            for nm, tabs in (("u", peer_u), ("v", peer_v)):
                for i in range(NL):
                    sap = nc.dram_tensor(f"tbf_{nm}{i}", [16384, D], BF16, kind="Internal").ap()
                    TBF[(nm, i)] = sap
                    src_v = tabs[i].rearrange("(c p j) d -> c p j d", p=P, j=4)
                    dst_v = sap.rearrange("(c p j) d -> c p j d", p=P, j=4)
                    for c in range(16384 // (P * 4)):
                        a = CST[kk % 2]; b = CBF[kk % 2]
                        LD(a, a[:], src_v[c])
                        if kk % 2 == 0:
                            V(lambda e, a=a, b=b: e.tensor_copy(out=b[:], in_=a[:]), [a], [b])
                        else:
                            A(lambda e, a=a, b=b: e.copy(out=b[:], in_=a[:]), [a], [b])
                        pg.dma(pg.pool, lambda e, b=b, c=c, dst_v=dst_v: e.dma_start(out=dst_v[c], in_=b[:]), src=b)
                        kk += 1
        pg.barrier()

        stg_i = [0]

        def load_cast(dst, dst_ap, src_ap, ncols):
            sb = STG[stg_i[0] % 2]; k = stg_i[0]; stg_i[0] += 1
            LD(sb, sb[:, 0:ncols], src_ap)
            if k % 2 == 0:
                V(lambda e: e.tensor_copy(out=dst_ap, in_=sb[:, 0:ncols]), [sb], [dst])
            else:
                A(lambda e: e.copy(out=dst_ap, in_=sb[:, 0:ncols]), [sb], [dst])

        def layer_norm(src_ap, src_bufs, dst_ap, dst_buf):
            for c in range(2):
                V(lambda e, c=c: e.bn_stats(out=STt[:, c, :], in_=src_ap[:, c * 512:(c + 1) * 512]), src_bufs, [STt])
            V(lambda e: e.bn_aggr(out=MV[:], in_=STt[:]), [STt], [MV])
            V(lambda e: e.tensor_scalar(out=RS[:], in0=MV[:, 1:2], scalar1=LN_EPS, scalar2=None, op0=ALU.add), [MV], [RS])
            A(lambda e: e.sqrt(out=RS[:], in_=RS[:]), [RS], [RS])
            V(lambda e: e.reciprocal(out=RS[:], in_=RS[:]), [RS], [RS])
            V(lambda e: e.tensor_scalar(out=dst_ap, in0=src_ap, scalar1=MV[:, 0:1], scalar2=RS[:, 0:1],
                                        op0=ALU.subtract, op1=ALU.mult), list(src_bufs) + [MV, RS], [dst_buf])
            G(lambda e: e.tensor_tensor(out=dst_ap, in0=dst_ap, in1=Gt[:], op=ALU.mult), [dst_buf, Gt], [dst_buf])
            G(lambda e: e.tensor_tensor(out=dst_ap, in0=dst_ap, in1=Bt[:], op=ALU.add), [dst_buf, Bt], [dst_buf])

        def load_ln_params(g_ap, b_ap):
            LD(Gt, Gt[:], g_ap.partition_broadcast(P))
            LD(Bt, Bt[:], b_ap.partition_broadcast(P))

        def make_hT(hb):
            for c in range(8):
                ps = PS0 if c < 4 else PS1
                T(lambda e, c=c, ps=ps: e.transpose(out=ps[:, (c % 4) * P:(c % 4 + 1) * P], in_=hb[:, c * P:(c + 1) * P], identity=ID[:]),
                  [hb, ID], [ps])
            V(lambda e: e.tensor_copy(out=HTt[:, 0:4, :], in_=PS0[:].rearrange("p (c f) -> p c f", c=4)), [PS0], [HTt])
            A(lambda e: e.copy(out=HTt[:, 4:8, :], in_=PS1[:].rearrange("p (c f) -> p c f", c=4)), [PS1], [HTt])

        def rms_rope(ph, src, src_buf, nh, gain, ti, dst_view, dst_buf, sc):
            SQ, SSq, XN, TA, TB = sc["SQ"], sc["SSq"], sc["XN"], sc["TA"], sc["TB"]
            V(lambda e: e.tensor_tensor(out=SQ[:, 0:nh, :], in0=src, in1=src, op=ALU.mult), [src_buf], [SQ])
            V(lambda e: e.tensor_reduce(out=SSq[:, 0:nh], in_=SQ[:, 0:nh, :], axis=AX.X, op=ALU.add), [SQ], [SSq])
            V(lambda e: e.tensor_scalar(out=SSq[:, 0:nh], in0=SSq[:, 0:nh], scalar1=1.0 / 64, scalar2=RMS_EPS, op0=ALU.mult, op1=ALU.add), [SSq], [SSq])
            A(lambda e: e.sqrt(out=SSq[:, 0:nh], in_=SSq[:, 0:nh]), [SSq], [SSq])
            V(lambda e: e.reciprocal(out=SSq[:, 0:nh], in_=SSq[:, 0:nh]), [SSq], [SSq])
            V(lambda e: e.tensor_tensor(out=XN[:, 0:nh, :], in0=src, in1=SSq[:, 0:nh].unsqueeze(2).to_broadcast([P, nh, 64]), op=ALU.mult), [src_buf, SSq], [XN])
            G(lambda e: e.tensor_tensor(out=XN[:, 0:nh, :], in0=XN[:, 0:nh, :], in1=gain[:, :].unsqueeze(1).to_broadcast([P, nh, 64]), op=ALU.mult), [XN, gain], [XN])
            xv = XN[:, 0:nh, :].rearrange("p h (a b f) -> p h a b f", a=2, b=2)
            x1 = xv[:, :, :, 0, :]; x2 = xv[:, :, :, 1, :]
            cb = COS[:, ti, :].rearrange("p (a f) -> p a f", a=2).unsqueeze(1).to_broadcast([P, nh, 2, 16])
            sb_ = SIN[:, ti, :].rearrange("p (a f) -> p a f", a=2).unsqueeze(1).to_broadcast([P, nh, 2, 16])
            dv = dst_view.rearrange("p h (a b f) -> p h a b f", a=2, b=2)
            ta = TA[:, 0:nh, :].rearrange("p h (a f) -> p h a f", a=2)
            tb = TB[:, 0:nh, :].rearrange("p h (a f) -> p h a f", a=2)
            V(lambda e: e.tensor_tensor(out=ta, in0=x1, in1=cb, op=ALU.mult), [XN, COS], [TA])
            G(lambda e: e.tensor_tensor(out=tb, in0=x2, in1=sb_, op=ALU.mult), [XN, SIN], [TB])
            V(lambda e: e.tensor_tensor(out=dv[:, :, :, 0, :], in0=ta, in1=tb, op=ALU.subtract), [TA, TB], [dst_buf])
            V(lambda e: e.tensor_tensor(out=ta, in0=x1, in1=sb_, op=ALU.mult), [XN, SIN], [TA])
            G(lambda e: e.tensor_tensor(out=tb, in0=x2, in1=cb, op=ALU.mult), [XN, COS], [TB])
            V(lambda e: e.tensor_tensor(out=dv[:, :, :, 1, :], in0=ta, in1=tb, op=ALU.add), [TA, TB], [dst_buf])

        def attention_layer(l):
            with ExitStack() as ph:
                KT_A = pg.sbuf("KT_A", [P, 2, S], BF16, ph)
                KT_B = pg.sbuf("KT_B", [P, 2, S], BF16, ph)
                VA = pg.sbuf("VA", [P, NT, 2, 65], BF16, ph)
                VB = pg.sbuf("VB", [P, NT, 2, 64], BF16, ph)
                GQ = pg.sbuf("GQ", [P, 64], F32, ph)
                GK = pg.sbuf("GK", [P, 64], F32, ph)
                SINK = pg.sbuf("SINK", [P, 8], F32, ph)
                GOT = pg.sbuf("GOT", [P, 8], F32, ph)
                sc = dict(SQ=pg.sbuf("SQ", [P, 8, 64], F32, ph), SSq=pg.sbuf("SSq", [P, 8], F32, ph),
                          XN=pg.sbuf("XN", [P, 8, 64], F32, ph), TA=pg.sbuf("TA", [P, 8, 32], F32, ph),
                          TB=pg.sbuf("TB", [P, 8, 32], F32, ph))
                LD(GQ, GQ[:], qn_g[l:l + 1, :].partition_broadcast(P))
                V(lambda e: e.tensor_scalar(out=GQ[:], in0=GQ[:], scalar1=0.125, scalar2=None, op0=ALU.mult), [GQ], [GQ])
                LD(GK, GK[:], kn_g[l:l + 1, :].partition_broadcast(P))
                LD(SINK, SINK[:], sink[l:l + 1, :].partition_broadcast(P))
                LD(GOT, GOT[:], gnT[l])
                V(lambda e: e.memset(VA[:], 1.0), [], [VA])
                with ExitStack() as sp:
                    WKV = pg.sbuf("WKV", [P, 8, 512], BF16, sp)
                    KV32 = pg.sbuf("KV32", [P, 512], F32, sp)
                    KAb = pg.sbuf("KAb", [P, 2, 2, 64], BF16, sp)
                    KBb = pg.sbuf("KBb", [P, 2, 2, 64], BF16, sp)
                    for kc in range(8):
                        rows = slice(kc * P, (kc + 1) * P)
                        load_cast(WKV, WKV[:, kc, 0:256], w_in[l, rows, 512:768], 256)
                        load_cast(WKV, WKV[:, kc, 256:512], w_in[l, rows, 1280:1536], 256)
                    for ti in range(NT):
                        make_hT(H[ti])
                        for kc in range(8):
                            T(lambda e, kc=kc: e.matmul(PS6[:], lhsT=HTt[:, kc, :], rhs=WKV[:, kc, :], start=(kc == 0), stop=(kc == 7)),
                              [HTt, WKV], [PS6])
                        A(lambda e: e.copy(out=KV32[:], in_=PS6[:]), [PS6], [KV32])
                        ka = KV32[:, 0:128].rearrange("p (h d) -> p h d", h=2)
                        rms_rope(sp, ka, KV32, 2, GK, ti, KAb[:, :, 0, :], KAb, sc)
                        V(lambda e: e.tensor_copy(out=KAb[:, :, 1, :], in_=KAb[:, :, 0, :]), [KAb], [KAb])
                        kb = KV32[:, 256:384].rearrange("p (h d) -> p h d", h=2)
                        for dup in range(2):
                            V(lambda e, dup=dup: e.tensor_copy(out=KBb[:, :, dup, :], in_=kb), [KV32], [KBb])
                        A(lambda e, ti=ti: e.copy(out=VA[:, ti, :, 0:64], in_=KV32[:, 128:256].rearrange("p (h d) -> p h d", h=2)), [KV32], [VA])
                        A(lambda e, ti=ti: e.copy(out=VB[:, ti, :, :], in_=KV32[:, 384:512].rearrange("p (h d) -> p h d", h=2)), [KV32], [VB])
                        for kv in range(2):
                            T(lambda e, kv=kv: e.transpose(out=PSB[:, kv * P:(kv + 1) * P], in_=KAb[:, kv, :, :].rearrange("p a d -> p (a d)"), identity=IDb[:]),
                              [KAb, IDb], [PSB])
                            T(lambda e, kv=kv: e.transpose(out=PSB[:, (2 + kv) * P:(3 + kv) * P], in_=KBb[:, kv, :, :].rearrange("p a d -> p (a d)"), identity=IDb[:]),
                              [KBb, IDb], [PSB])
                        V(lambda e, ti=ti: e.tensor_copy(out=KT_A[:, :, ti * P:(ti + 1) * P], in_=PSB[:, 0:256].rearrange("p (k t) -> p k t", k=2)), [PSB], [KT_A])
                        V(lambda e, ti=ti: e.tensor_copy(out=KT_B[:, :, ti * P:(ti + 1) * P], in_=PSB[:, 256:512].rearrange("p (k t) -> p k t", k=2)), [PSB], [KT_B])
                pg.barrier()
                with ExitStack() as sg:
                    WQ = pg.sbuf("WQ", [P, 8, 1024], BF16, sg)
                    WO = pg.sbuf("WO", [P, 8, 1024], BF16, sg)
                    QA32 = pg.sbuf("QA32", [P, 512], F32, sg)
                    QAb = pg.sbuf("QAb", [P, 512], BF16, sg)
                    QBb = pg.sbuf("QBb", [P, 512], BF16, sg)
                    QTt = pg.sbuf("QTt", [P, 8, P], BF16, sg)
                    PTb = [pg.sbuf(f"PTb{i}", [P, 1024], BF16, sg) for i in range(2)]
                    OALL = pg.sbuf("OALL", [P, 8, 65], F32, sg)
                    RZ = pg.sbuf("RZ", [P, 8], F32, sg)
                    OT = pg.sbuf("OT", [P, 16, 64], F32, sg)
                    OSQ = pg.sbuf("OSQ", [P, 16, 64], F32, sg)
                    OSS = pg.sbuf("OSS", [P, 16], F32, sg)
                    OTb = pg.sbuf("OTb", [P, 1024], BF16, sg)
                    OTt = pg.sbuf("OTt", [P, 8, P], BF16, sg)
                    SB = pg.sbuf("SB", [P, 8, 384], F32, sg)
                    PB = pg.sbuf("PB", [P, 8, 384], BF16, sg)
                    PTB = [pg.sbuf(f"PTB{i}", [P, 384], BF16, sg) for i in range(2)]
                    MX = pg.sbuf("MX", [P, 8], F32, sg)
                    NM = pg.sbuf("NM", [P, 8], F32, sg)
                    ES = pg.sbuf("ES", [P, 8], F32, sg)
                    RSUM = pg.sbuf("RSUM", [P, 8], F32, sg)
                    ZB = pg.sbuf("ZB", [P, 8], F32, sg)
                    for kc in range(8):
                        rows = slice(kc * P, (kc + 1) * P)
                        load_cast(WQ, WQ[:, kc, 0:512], w_in[l, rows, 0:512], 512)
                        load_cast(WQ, WQ[:, kc, 512:1024], w_in[l, rows, 768:1280], 512)
                        load_cast(WO, WO[:, kc, 0:512], w_o[l, rows, 0:512], 512)
                        load_cast(WO, WO[:, kc, 512:1024], w_o[l, rows, 512:1024], 512)
                    load_ln_params(ln1_g[l:l + 1, :], ln1_b[l:l + 1, :])
                    for qt in range(NT):
                        make_hT(H[qt])
                        for nb in range(2):
                            for kc in range(8):
                                T(lambda e, nb=nb, kc=kc: e.matmul(PS23[:, nb * 512:(nb + 1) * 512], lhsT=HTt[:, kc, :], rhs=WQ[:, kc, nb * 512:(nb + 1) * 512],
                                                                   start=(kc == 0), stop=(kc == 7)), [HTt, WQ], [PS23])
                        A(lambda e: e.copy(out=QA32[:], in_=PS23[:, 0:512]), [PS23], [QA32])
                        A(lambda e: e.mul(out=QBb[:], in_=PS23[:, 512:1024], mul=0.125), [PS23], [QBb])
                        rms_rope(sg, QA32[:].rearrange("p (h d) -> p h d", h=8), QA32, 8, GQ, qt,
                                 QAb[:].rearrange("p (h d) -> p h d", h=8), QAb, sc)
                        for hp in range(4):
                            T(lambda e, hp=hp: e.transpose(out=PSB[:, hp * P:(hp + 1) * P], in_=QAb[:, hp * P:(hp + 1) * P], identity=IDb[:]), [QAb, IDb], [PSB])
                            T(lambda e, hp=hp: e.transpose(out=PSB[:, (4 + hp) * P:(5 + hp) * P], in_=QBb[:, hp * P:(hp + 1) * P], identity=IDb[:]), [QBb, IDb], [PSB])
                        V(lambda e: e.tensor_copy(out=QTt[:], in_=PSB[:].rearrange("p (c t) -> p c t", c=8)), [PSB], [QTt])
                        blocks = [(h, hs) for h in range(8) for hs in range(2)]
                        SPS = [PS23, PS45]
                        OPS = [PS0, PS1]

                        def emit_S(b):
                            h, hs = blocks[b]
                            kv, half, hp = h // 4, h % 2, h // 2
                            sps = SPS[b % 2]
                            pr = slice(half * 64, (half + 1) * 64)
                            for j in range(8):
                                stt = hs * 8 + j
                                T(lambda e, j=j, stt=stt: e.matmul(sps[:, j * P:(j + 1) * P], lhsT=KT_A[pr, kv, stt * P:(stt + 1) * P], rhs=QTt[pr, hp, :],
                                                                     start=True, stop=True), [KT_A, QTt], [sps])
                            A(lambda e: e.activation(out=PTb[b % 2][:], in_=sps[:], func=AF.Exp), [sps], [PTb[b % 2]])

                        def emit_PV(b):
                            h, hs = blocks[b]
                            kv = h // 4
                            ops = OPS[h // 4]
                            g = h % 4
                            for j in range(8):
                                stt = hs * 8 + j
                                T(lambda e, j=j, stt=stt: e.matmul(ops[:, g * 65:(g + 1) * 65], lhsT=PTb[b % 2][:, j * P:(j + 1) * P], rhs=VA[:, stt, kv, :],
                                                                     start=(stt == 0), stop=(stt == 15)), [PTb[b % 2], VA], [ops])
                            if hs == 1 and g == 3:
                                A(lambda e: e.copy(out=OALL[:, (h // 4) * 4:(h // 4) * 4 + 4, :], in_=ops[:, 0:260].rearrange("p (g c) -> p g c", g=4)), [ops], [OALL])

                        emit_S(0)
                        for b in range(len(blocks)):
                            if b + 1 < len(blocks):
                                emit_S(b + 1)
                            emit_PV(b)
                        V(lambda e: e.reciprocal(out=RZ[:], in_=OALL[:, :, 64]), [OALL], [RZ])
                        V(lambda e: e.tensor_tensor(out=OT[:, 0:8, :], in0=OALL[:, :, 0:64], in1=RZ[:, :].unsqueeze(2).to_broadcast([P, 8, 64]), op=ALU.mult), [OALL, RZ], [OT])
                        lo = max(qt - 1, 0); hi = min(qt + 1, NT - 1); nk = hi - lo + 1; ncol = nk * P
                        c0 = (lo - (qt - 1)) * P
                        for h in range(8):
                            kv, half, hp = h // 4, h % 2, h // 2
                            pr = slice(half * 64, (half + 1) * 64)
                            psb = PS6 if h % 2 == 0 else PS7
                            T(lambda e, psb=psb, kv=kv, hp=hp, pr=pr: e.matmul(psb[:, 0:ncol], lhsT=QTt[pr, 4 + hp, :], rhs=KT_B[pr, kv, lo * P:(hi + 1) * P], start=True, stop=True),
                              [QTt, KT_B], [psb])
                            slope = float(2.0 ** (-(h + 1)))
                            V(lambda e, psb=psb, h=h, slope=slope: e.scalar_tensor_tensor(out=SB[:, h, 0:ncol], in0=NEGD[:, c0:c0 + ncol], scalar=slope, in1=psb[:, 0:ncol],
                                                                                           op0=ALU.mult, op1=ALU.add), [NEGD, psb], [SB])
                        V(lambda e: e.tensor_reduce(out=MX[:], in_=SB[:, :, 0:ncol], axis=AX.X, op=ALU.max), [SB], [MX])
                        V(lambda e: e.tensor_tensor(out=MX[:], in0=MX[:], in1=SINK[:], op=ALU.max), [MX, SINK], [MX])
                        V(lambda e: e.tensor_scalar(out=NM[:], in0=MX[:], scalar1=-1.0, scalar2=None, op0=ALU.mult), [MX], [NM])
                        V(lambda e: e.tensor_tensor(out=ES[:], in0=SINK[:], in1=MX[:], op=ALU.subtract), [SINK, MX], [ES])
                        A(lambda e: e.activation(out=ES[:], in_=ES[:], func=AF.Exp), [ES], [ES])
                        for h in range(8):
                            A(lambda e, h=h: e.activation(out=PB[:, h, 0:ncol], in_=SB[:, h, 0:ncol], func=AF.Exp, bias=NM[:, h:h + 1], scale=1.0,
                                                          accum_out=RSUM[:, h:h + 1]), [SB, NM], [PB, RSUM])
                        V(lambda e: e.tensor_tensor(out=ZB[:], in0=RSUM[:], in1=ES[:], op=ALU.add), [RSUM, ES], [ZB])
                        V(lambda e: e.reciprocal(out=ZB[:], in_=ZB[:]), [ZB], [ZB])
                        for h in range(8):
                            kv = h // 4
                            ptb = PTB[h % 2]
                            for j in range(nk):
                                T(lambda e, h=h, j=j: e.transpose(out=PSB[:, j * P:(j + 1) * P], in_=PB[:, h, j * P:(j + 1) * P], identity=IDb[:]), [PB, IDb], [PSB])
                            V(lambda e, ptb=ptb: e.tensor_copy(out=ptb[:, 0:ncol], in_=PSB[:, 0:ncol]), [PSB], [ptb])
                            for j in range(nk):
                                T(lambda e, h=h, j=j, ptb=ptb, kv=kv: e.matmul(PS1[:, h * 64:(h + 1) * 64], lhsT=ptb[:, j * P:(j + 1) * P], rhs=VB[:, lo + j, kv, :],
                                                                                 start=(j == 0), stop=(j == nk - 1)), [ptb, VB], [PS1])
                        V(lambda e: e.tensor_tensor(out=OT[:, 8:16, :], in0=PS1[:].rearrange("p (h d) -> p h d", h=8), in1=ZB[:, :].unsqueeze(2).to_broadcast([P, 8, 64]), op=ALU.mult),
                          [PS1, ZB], [OT])
                        V(lambda e: e.tensor_tensor(out=OSQ[:], in0=OT[:], in1=OT[:], op=ALU.mult), [OT], [OSQ])
                        V(lambda e: e.tensor_reduce(out=OSS[:], in_=OSQ[:], axis=AX.X, op=ALU.add), [OSQ], [OSS])
                        V(lambda e: e.tensor_scalar(out=OSS[:], in0=OSS[:], scalar1=1.0 / 64, scalar2=RMS_EPS, op0=ALU.mult, op1=ALU.add), [OSS], [OSS])
                        A(lambda e: e.sqrt(out=OSS[:], in_=OSS[:]), [OSS], [OSS])
                        V(lambda e: e.reciprocal(out=OSS[:], in_=OSS[:]), [OSS], [OSS])
                        V(lambda e: e.tensor_tensor(out=OTb[:].rearrange("p (h d) -> p h d", h=16), in0=OT[:], in1=OSS[:, :].unsqueeze(2).to_broadcast([P, 16, 64]), op=ALU.mult),
                          [OT, OSS], [OTb])
                        for kc in range(8):
                            T(lambda e, kc=kc: e.transpose(out=PSB[:, kc * P:(kc + 1) * P], in_=OTb[:, kc * P:(kc + 1) * P], identity=IDb[:]), [OTb, IDb], [PSB])
                        V(lambda e: e.tensor_tensor(out=OTt[:], in0=PSB[:].rearrange("p (c t) -> p c t", c=8), in1=GOT[:, :].unsqueeze(2).to_broadcast([P, 8, P]), op=ALU.mult),
                          [PSB, GOT], [OTt])
                        for nb in range(2):
                            for kc in range(8):
                                T(lambda e, nb=nb, kc=kc: e.matmul(PS45[:, nb * 512:(nb + 1) * 512], lhsT=OTt[:, kc, :], rhs=WO[:, kc, nb * 512:(nb + 1) * 512],
                                                                   start=(kc == 0), stop=(kc == 7)), [OTt, WO], [PS45])
                        V(lambda e, qt=qt: e.scalar_tensor_tensor(out=R[:], in0=H[qt][:], scalar=ALPHA, in1=PS45[:], op0=ALU.mult, op1=ALU.add), [H[qt], PS45], [R])
                        layer_norm(R[:], [R], H[qt][:], H[qt])
                pg.barrier()

        def peer_layer(l, seq, last):
            NS = peer_slots
            with ExitStack() as ph:
                EIDX = pg.sbuf("EIDX", [P, NT, 128], I32, ph)
                GW = pg.sbuf("GW", [P, NT, 128], F32, ph)
                with ExitStack() as s1:
                    WQp = pg.sbuf("WQp", [P, 8, 2048], BF16, s1)
                    KEYST = pg.sbuf("KEYST", [P, 16, P], BF16, s1)
                    K32 = pg.sbuf("K32", [P, P], F32, s1)
                    QTp = pg.sbuf("QTp", [P, 16, P], BF16, s1)
                    SC = pg.sbuf("SC", [P, 16, P], F32, s1)
                    SC2 = pg.sbuf("SC2", [P, 16, P], F32, s1)
                    S16 = pg.sbuf("S16", [P, 16, 16], F32, s1)
                    I16u = pg.sbuf("I16u", [P, 16, 16], U32, s1)
                    I16f = pg.sbuf("I16f", [P, 16, 16], F32, s1)
                    CAND = pg.sbuf("CAND", [P, 8, 256], F32, s1)
                    CAND2 = pg.sbuf("CAND2", [P, 8, 256], F32, s1)
                    TOPS = pg.sbuf("TOPS", [P, 8, 16], F32, s1)
                    POSu = pg.sbuf("POSu", [P, 8, 16], U32, s1)
                    POSf = pg.sbuf("POSf", [P, 8, 16], F32, s1)
                    AFl = pg.sbuf("AFl", [P, 8, 16], F32, s1)
                    BFl = pg.sbuf("BFl", [P, 8, 16], F32, s1)
                    OH = pg.sbuf("OH", [P, 8, 16, 16], F32, s1)
                    I1S = pg.sbuf("I1S", [P, 8, 16], F32, s1)
                    I2S = pg.sbuf("I2S", [P, 8, 16], F32, s1)
                    EF = pg.sbuf("EF", [P, 8, 16], F32, s1)
                    MXT = pg.sbuf("MXT", [P, 8], F32, s1)
                    TS = pg.sbuf("TS", [P, 8, 16], F32, s1)
                    SM = pg.sbuf("SM", [P, 8], F32, s1)
                    for kc in range(8):
                        rows = slice(kc * P, (kc + 1) * P)
                        for cc in range(4):
                            load_cast(WQp, WQp[:, kc, cc * 512:(cc + 1) * 512], peer_wq[l, rows, cc * 512:(cc + 1) * 512], 512)
                    for hp in range(16):
                        LD(K32, K32[:], peer_keys[l, hp])
                        T(lambda e: e.transpose(out=PS6[:, 0:P], in_=K32[:], identity=ID[:]), [K32, ID], [PS6])
                        V(lambda e, hp=hp: e.tensor_copy(out=KEYST[:, hp, :], in_=PS6[:, 0:P]), [PS6], [KEYST])
                    iota4 = IOTA[:, :].unsqueeze(1).unsqueeze(1).to_broadcast([P, 8, 16, 16])
                    for ti in range(NT):
                        make_hT(H[ti])
                        for hp in range(16):
                            ps = PS23 if (hp // 4) % 2 == 0 else PS45
                            for kc in range(8):
                                T(lambda e, hp=hp, kc=kc, ps=ps: e.matmul(ps[:, (hp % 4) * P:(hp % 4 + 1) * P], lhsT=WQp[:, kc, hp * P:(hp + 1) * P], rhs=HTt[:, kc, :],
                                                                          start=(kc == 0), stop=(kc == 7)), [WQp, HTt], [ps])
                            if hp % 4 == 3:
                                g4 = hp // 4
                                if g4 % 2 == 0:
                                    V(lambda e, g4=g4, ps=ps: e.tensor_copy(out=QTp[:, g4 * 4:g4 * 4 + 4, :], in_=ps[:, 0:512].rearrange("p (c t) -> p c t", c=4)), [ps], [QTp])
                                else:
                                    A(lambda e, g4=g4, ps=ps: e.copy(out=QTp[:, g4 * 4:g4 * 4 + 4, :], in_=ps[:, 0:512].rearrange("p (c t) -> p c t", c=4)), [ps], [QTp])
                        for hp in range(16):
                            ps = PS6 if (hp // 4) % 2 == 0 else PS0
                            T(lambda e, hp=hp, ps=ps: e.matmul(ps[:, (hp % 4) * P:(hp % 4 + 1) * P], lhsT=QTp[:, hp, :], rhs=KEYST[:, hp, :], start=True, stop=True),
                              [QTp, KEYST], [ps])
                            if hp % 4 == 3:
                                g4 = hp // 4
                                A(lambda e, g4=g4, ps=ps: e.copy(out=SC[:, g4 * 4:g4 * 4 + 4, :], in_=ps[:].rearrange("p (c t) -> p c t", c=4)), [ps], [SC])

                        def top16(vals, vals2, vbufs, outv, outi, obufs):
                            V(lambda e: e.max(out=outv[:, 0:8], in_=vals), vbufs[:1], [obufs[0]])
                            V(lambda e: e.max_index(out=outi[:, 0:8], in_max=outv[:, 0:8], in_values=vals), [vbufs[0], obufs[0]], [obufs[1]])
                            V(lambda e: e.match_replace(out=vals2, in_to_replace=outv[:, 0:8], in_values=vals, imm_value=-1e30), [vbufs[0], obufs[0]], [vbufs[1]])
                            V(lambda e: e.max(out=outv[:, 8:16], in_=vals2), [vbufs[1]], [obufs[0]])
                            V(lambda e: e.max_index(out=outi[:, 8:16], in_max=outv[:, 8:16], in_values=vals2), [vbufs[1], obufs[0]], [obufs[1]])

                        for hp in range(16):
                            top16(SC[:, hp, :], SC2[:, hp, :], [SC, SC2], S16[:, hp, :], I16u[:, hp, :], [S16, I16u])
                        s16v = S16[:].rearrange("p (h q) k -> p h q k", q=2)
                        V(lambda e: e.tensor_tensor(out=CAND[:].rearrange("p h (a b) -> p h a b", a=16),
                                                    in0=s16v[:, :, 0, :].unsqueeze(3).to_broadcast([P, 8, 16, 16]),
                                                    in1=s16v[:, :, 1, :].unsqueeze(2).to_broadcast([P, 8, 16, 16]), op=ALU.add), [S16], [CAND])
                        for h in range(8):
                            top16(CAND[:, h, :], CAND2[:, h, :], [CAND, CAND2], TOPS[:, h, :], POSu[:, h, :], [TOPS, POSu])
                        V(lambda e: e.tensor_copy(out=I16f[:], in_=I16u[:]), [I16u], [I16f])
                        V(lambda e: e.tensor_copy(out=POSf[:], in_=POSu[:]), [POSu], [POSf])
                        V(lambda e: e.tensor_scalar(out=AFl[:], in0=POSf[:], scalar1=0.0625, scalar2=-0.46875, op0=ALU.mult, op1=ALU.add), [POSf], [AFl])
                        V(lambda e: e.tensor_scalar(out=AFl[:], in0=AFl[:], scalar1=12582912.0, scalar2=-12582912.0, op0=ALU.add, op1=ALU.add), [AFl], [AFl])
                        V(lambda e: e.scalar_tensor_tensor(out=BFl[:], in0=AFl[:], scalar=-16.0, in1=POSf[:], op0=ALU.mult, op1=ALU.add), [AFl, POSf], [BFl])
                        i16v = I16f[:].rearrange("p (h q) k -> p h q k", q=2)
                        for (sel, qq, dst) in ((AFl, 0, I1S), (BFl, 1, I2S)):
                            V(lambda e, sel=sel: e.tensor_tensor(out=OH[:], in0=iota4, in1=sel[:, :, :].unsqueeze(3).to_broadcast([P, 8, 16, 16]), op=ALU.is_equal), [IOTA, sel], [OH])
                            V(lambda e, qq=qq: e.tensor_tensor(out=OH[:], in0=OH[:], in1=i16v[:, :, qq, :].unsqueeze(2).to_broadcast([P, 8, 16, 16]), op=ALU.mult), [OH, I16f], [OH])
                            V(lambda e, dst=dst: e.tensor_reduce(out=dst[:], in_=OH[:], axis=AX.X, op=ALU.add), [OH], [dst])
                        V(lambda e: e.scalar_tensor_tensor(out=EF[:], in0=I1S[:], scalar=128.0, in1=I2S[:], op0=ALU.mult, op1=ALU.add), [I1S, I2S], [EF])
                        V(lambda e, ti=ti: e.tensor_copy(out=EIDX[:, ti, :].rearrange("p (h k) -> p h k", h=8), in_=EF[:]), [EF], [EIDX])
                        V(lambda e: e.tensor_reduce(out=MXT[:], in_=TOPS[:], axis=AX.X, op=ALU.max), [TOPS], [MXT])
                        V(lambda e: e.tensor_tensor(out=TS[:], in0=TOPS[:], in1=MXT[:, :].unsqueeze(2).to_broadcast([P, 8, 16]), op=ALU.subtract), [TOPS, MXT], [TS])
                        A(lambda e: e.activation(out=TS[:], in_=TS[:], func=AF.Exp), [TS], [TS])
                        V(lambda e: e.tensor_reduce(out=SM[:], in_=TS[:], axis=AX.X, op=ALU.add), [TS], [SM])
                        V(lambda e: e.reciprocal(out=SM[:], in_=SM[:]), [SM], [SM])
                        V(lambda e, ti=ti: e.tensor_tensor(out=GW[:, ti, :].rearrange("p (h k) -> p h k", h=8), in0=TS[:], in1=SM[:, :].unsqueeze(2).to_broadcast([P, 8, 16]), op=ALU.mult),
                          [TS, SM], [GW])
                        if debug and ti == 0 and l == 0:
                            pg.dma(pg.sp, lambda e: e.dma_start(out=dbg["sc"], in_=SC[:].rearrange("p a b -> p (a b)")), src=SC)
                            pg.dma(pg.sp, lambda e: e.dma_start(out=dbg["s16"], in_=S16[:].rearrange("p a b -> p (a b)")), src=S16)
                            pg.dma(pg.sp, lambda e: e.dma_start(out=dbg["i16"], in_=I16f[:].rearrange("p a b -> p (a b)")), src=I16f)
                            pg.dma(pg.sp, lambda e: e.dma_start(out=dbg["tops"], in_=TOPS[:].rearrange("p a b -> p (a b)")), src=TOPS)
                            pg.dma(pg.sp, lambda e: e.dma_start(out=dbg["pos"], in_=POSf[:].rearrange("p a b -> p (a b)")), src=POSf)
                            pg.dma(pg.sp, lambda e: e.dma_start(out=dbg["eidx"], in_=EIDX[:, 0, :]), src=EIDX)
                            pg.dma(pg.sp, lambda e: e.dma_start(out=dbg["gw"], in_=GW[:, 0, :]), src=GW)
                            pg.dma(pg.sp, lambda e: e.dma_start(out=dbg["h"], in_=H[0][:]), src=H[0])
                pg.barrier()
                with ExitStack() as s2:
                    GS = 4
                    NB = 3
                    UB = [pg.sbuf(f"UB{i}", [P, GS, D], BF16, s2) for i in range(NB)]
                    VBf = [pg.sbuf(f"VBf{i}", [P, GS, D], BF16, s2) for i in range(NB)]
                    JUNK = pg.sbuf("JUNK", [P, D], F32, s2)
                    ADOT = pg.sbuf("ADOT", [P, 128], F32, s2)
                    WGT = pg.sbuf("WGT", [P, 128], F32, s2)
                    ACC = pg.sbuf("ACC", [P, D], F32, s2)
                    load_ln_params(ln2_g[l:l + 1, :], ln2_b[l:l + 1, :])
                    ng = NS // GS
                    for ti in range(NT):
                        if NS < 128:
                            V(lambda e: e.memset(ADOT[:], 0.0), [], [ADOT])
                        for tab, bufs, which in ((TBF[("u", l)], UB, 0), (TBF[("v", l)], VBf, 1)):
                            if which == 1:
                                A(lambda e: e.activation(out=WGT[:], in_=ADOT[:], func=AF.Gelu), [ADOT], [WGT])
                                V(lambda e, ti=ti: e.tensor_tensor(out=WGT[:], in0=WGT[:], in1=GW[:, ti, :], op=ALU.mult), [WGT, GW], [WGT])
                            for grp in range(ng):
                                bb = bufs[grp % NB]
                                for jj in range(GS):
                                    j = grp * GS + jj
                                    pg.dma(pg.pool, lambda e, bb=bb, jj=jj, j=j, ti=ti, tab=tab: e.indirect_dma_start(
                                        out=bb[:, jj, :], out_offset=None, in_=tab,
                                        in_offset=bass.IndirectOffsetOnAxis(ap=EIDX[:, ti, j:j + 1], axis=0)), dst=bb, extra_reads=[EIDX])
                                for jj in range(GS):
                                    j = grp * GS + jj
                                    if which == 0:
                                        V(lambda e, bb=bb, jj=jj, j=j, ti=ti: e.scalar_tensor_tensor(out=JUNK[:], in0=bb[:, jj, :], scalar=1.0, in1=H[ti][:], op0=ALU.mult, op1=ALU.mult,
                                                                                                     accum_out=ADOT[:, j:j + 1]), [bb, H[ti]], [JUNK, ADOT])
                                    elif j == 0:
                                        V(lambda e, bb=bb, jj=jj, j=j: e.tensor_scalar(out=ACC[:], in0=bb[:, jj, :], scalar1=WGT[:, j:j + 1], scalar2=None, op0=ALU.mult), [bb, WGT], [ACC])
                                    else:
                                        V(lambda e, bb=bb, jj=jj, j=j: e.scalar_tensor_tensor(out=ACC[:], in0=bb[:, jj, :], scalar=WGT[:, j:j + 1], in1=ACC[:], op0=ALU.mult, op1=ALU.add),
                                          [bb, WGT, ACC], [ACC])
                        if debug and ti == 0 and l == 0:
                            pg.dma(pg.sp, lambda e: e.dma_start(out=dbg["adot"], in_=ADOT[:]), src=ADOT)
                            pg.dma(pg.sp, lambda e: e.dma_start(out=dbg["wgt"], in_=WGT[:]), src=WGT)
                            pg.dma(pg.sp, lambda e: e.dma_start(out=dbg["acc"], in_=ACC[:]), src=ACC)
                        V(lambda e, ti=ti: e.scalar_tensor_tensor(out=R[:], in0=H[ti][:], scalar=ALPHA, in1=ACC[:], op0=ALU.mult, op1=ALU.add), [H[ti], ACC], [R])
                        layer_norm(R[:], [R], H[ti][:], H[ti])
                        if last:
                            pg.dma(pg.sp, lambda e, ti=ti: e.dma_start(out=out[seq, ti * P:(ti + 1) * P, :], in_=H[ti][:]), src=H[ti])
                pg.barrier()

        XT = [pg.sbuf(f"XT{i}", [P, D], F32) for i in range(2)]
        for seq in range(nseq):
            load_ln_params(ln_in_g, ln_in_b)
            for ti in range(NT):
                xb = XT[ti % 2]
                LD(xb, xb[:], x[seq, ti * P:(ti + 1) * P, :])
                layer_norm(xb[:], [xb], H[ti][:], H[ti])
            for l in range(n_layers):
                last = (l == n_layers - 1)
                if do_attn:
                    attention_layer(l)
                if do_peer:
                    peer_layer(l, seq, last)
                elif last:
                    for ti in range(NT):
                        pg.dma(pg.sp, lambda e, ti=ti: e.dma_start(out=out[seq, ti * P:(ti + 1) * P, :], in_=H[ti][:]), src=H[ti])
        pg.finish()
        print("ninst", pg.ninst(), {e.name: e.ninst for e in pg.engs}, flush=True)
    return nc


def make_consts():
    ident = np.eye(P, dtype=np.float32)
    rows = S // 64
    row = np.repeat(np.arange(rows), 64)
    col = np.tile(np.arange(64), rows)
    pos = np.stack([row, col], -1).astype(np.float32)
    inv_freq = (np.float32(10000.0) ** (-np.arange(16, dtype=np.float32) / np.float32(16))).astype(np.float32)
    ang = (pos[:, :, None] * inv_freq).astype(np.float32)
    cos = np.cos(ang).astype(np.float32).reshape(NT, P, 32).transpose(1, 0, 2)
    sin = np.sin(ang).astype(np.float32).reshape(NT, P, 32).transpose(1, 0, 2)
    p = np.arange(P)[:, None]; c = np.arange(384)[None, :]
    dist = np.abs(p + 128 - c).astype(np.float32)
    negd = np.where(dist <= 128, -dist, np.float32(-1e30)).astype(np.float32)
    iota = np.tile(np.arange(16, dtype=np.float32)[None, :], (P, 1))
    return dict(c_ident=ident, c_cos=np.ascontiguousarray(cos), c_sin=np.ascontiguousarray(sin),
                c_negd=np.ascontiguousarray(negd), c_iota=np.ascontiguousarray(iota))


def prep_weights(inp):
    f = lambda a: np.ascontiguousarray(np.asarray(a, dtype=np.float32))
    gn = np.concatenate([np.asarray(inp["gn_a_g"]), np.asarray(inp["gn_b_g"])], -1)
    gnT = np.ascontiguousarray(gn.reshape(NL, 8, P).transpose(0, 2, 1)).astype(np.float32)
    d = dict(
        ln_in_g=f(inp["ln_in_g"]).reshape(1, D), ln_in_b=f(inp["ln_in_b"]).reshape(1, D),
        w_in=f(inp["w_in"]), qn_g=f(inp["qn_g"]), kn_g=f(inp["kn_g"]), sink=f(inp["sink"]), gnT=gnT,
        w_o=f(inp["w_o"]), ln1_g=f(inp["ln1_g"]), ln1_b=f(inp["ln1_b"]), peer_wq=f(inp["peer_wq"]),
        peer_keys=f(inp["peer_keys"]).reshape(NL, 16, P, P),
        ln2_g=f(inp["ln2_g"]), ln2_b=f(inp["ln2_b"]))
    pu = f(inp["peer_u"]); pv = f(inp["peer_v"])
    for i in range(NL):
        d[f"peer_u{i}"] = pu[i]
        d[f"peer_v{i}"] = pv[i]
    d.update(make_consts())
    return d


def kernel(**inputs):
    x = np.asarray(inputs["x"], dtype=np.float32)
    B = x.shape[0]
    nseq = B // NCORES
    wd = prep_weights(inputs)
    nc = build(nseq=nseq)
    in_maps = []
    for c in range(NCORES):
        m = dict(wd)
        m["x"] = np.ascontiguousarray(x[c * nseq:(c + 1) * nseq])
        in_maps.append(m)
    res = run_bass_kernel_spmd(nc, in_maps, core_ids=list(range(NCORES)))
    return np.concatenate([np.asarray(r["out"]) for r in res.results], axis=0).astype(np.float32)
```
